# Optimizing a Trainium2 kernel written in Bass

```python
import jax, jax.numpy as jnp
from jax import lax
import numpy as np

D_MODEL = 2048
BATCH = 4
SEQ = 4096
DEPTH = 2

CHUNK = 64
EPS = 1e-6
MIX_HALF = D_MODEL // 2

CONV_WIDTH = 3
CONV_DIM = MIX_HALF
ATT_HEADS = 8
ATT_HEAD_DIM = MIX_HALF // ATT_HEADS
ATT_PAST_CHUNKS = 8
ATT_BAND = (ATT_PAST_CHUNKS + 1) * CHUNK
REL_CLIP = 256
REL_TABLE = (CHUNK - 1) + REL_CLIP + 1

HGRN_HEADS = 8
HGRN_EXPAND = 128
HGRN_DK_TOTAL = HGRN_HEADS * HGRN_EXPAND
HGRN_DV_TOTAL = MIX_HALF
HGRN_DV = HGRN_DV_TOTAL // HGRN_HEADS
GLA_HEADS = 4
GLA_DV_TOTAL = MIX_HALF
GLA_DK_TOTAL = GLA_DV_TOTAL // 2
GLA_DK = GLA_DK_TOTAL // GLA_HEADS
GLA_DV = GLA_DV_TOTAL // GLA_HEADS
GLA_GATE_RANK = 16
GLA_GATE_NORMALIZER = 16.0

EVEN_SIZES = (CONV_DIM, CONV_DIM, CONV_DIM, MIX_HALF, MIX_HALF, MIX_HALF)
ODD_SIZES = (HGRN_DK_TOTAL, HGRN_DK_TOTAL, HGRN_DV_TOTAL, HGRN_DV_TOTAL,
             GLA_DK_TOTAL, GLA_DK_TOTAL, GLA_DV_TOTAL, GLA_DV_TOTAL, GLA_GATE_RANK)
EVEN_IN = 3 * CONV_DIM + 3 * MIX_HALF
ODD_IN = 2 * HGRN_DK_TOTAL + 2 * HGRN_DV_TOTAL + 2 * GLA_DK_TOTAL + 2 * GLA_DV_TOTAL + GLA_GATE_RANK
D_FF = 4 * D_MODEL
N_EVEN = (DEPTH + 1) // 2
N_ODD = DEPTH // 2

kernel_name = "hybrid_conv_chunkattn_hgrn2_gla_block"


def rmsnorm(x, g):
    x32 = x.astype(jnp.float32)
    y = x32 * lax.rsqrt(jnp.mean(x32 * x32, axis=-1, keepdims=True) + EPS)
    return (y * g.astype(jnp.float32)).astype(x.dtype)


def split_cols(t, sizes):
    idx = [int(v) for v in np.cumsum(sizes)[:-1]]
    return jnp.split(t, idx, axis=-1)


def short_conv_mixer(b_gate, c_gate, h, conv_w):
    u = c_gate * h
    s = u.shape[1]
    up = jnp.pad(u, ((0, 0), (CONV_WIDTH - 1, 0), (0, 0)))
    y = conv_w[0] * up[:, 0:s]
    for j in range(1, CONV_WIDTH):
        y = y + conv_w[j] * up[:, j:j + s]
    return b_gate * y


def chunked_band_attention(q, k, v, rel_bias):
    b, s, h, dh = q.shape
    nc = s // CHUNK
    pad = ((0, 0), (ATT_PAST_CHUNKS * CHUNK, 0), (0, 0), (0, 0))
    kp = jnp.pad(k, pad).reshape(b, nc + ATT_PAST_CHUNKS, CHUNK, h, dh)
    vp = jnp.pad(v, pad).reshape(b, nc + ATT_PAST_CHUNKS, CHUNK, h, dh)
    k_band = jnp.concatenate([kp[:, j:j + nc] for j in range(ATT_PAST_CHUNKS + 1)], axis=2)
    v_band = jnp.concatenate([vp[:, j:j + nc] for j in range(ATT_PAST_CHUNKS + 1)], axis=2)
    qc = q.reshape(b, nc, CHUNK, h, dh)
    scores = jnp.einsum('bnqhd,bnkhd->bhnqk', qc, k_band).astype(jnp.float32) * (dh ** -0.5)
    qi = np.arange(CHUNK)[:, None]
    ki = np.arange(ATT_BAND)[None, :]
    rel = ATT_PAST_CHUNKS * CHUNK + qi - ki
    idx = np.clip(rel, -(CHUNK - 1), REL_CLIP) + (CHUNK - 1)
    bias = rel_bias[:, idx].astype(jnp.float32)
    valid = (np.arange(nc)[:, None] + ki // CHUNK) >= ATT_PAST_CHUNKS
    scores = scores + bias[None, :, None]
    scores = jnp.where(valid[None, None, :, None, :], scores, -1e30)
    p = jax.nn.softmax(scores, axis=-1).astype(v.dtype)
    o = jnp.einsum('bhnqk,bnkhd->bnqhd', p, v_band)
    return o.reshape(b, s, h * dh)


def chunk_gated_recurrence(q, k, v, log_a):
    out_dtype = v.dtype
    b, s, h, dk = q.shape
    dv = v.shape[-1]
    nc = s // CHUNK

    def to_chunks(t):
        return t.astype(jnp.float32).reshape(b, nc, CHUNK, h, t.shape[-1]).transpose(1, 0, 3, 2, 4)

    qc, kc, vc = to_chunks(q), to_chunks(k), to_chunks(v)
    bc = jnp.cumsum(to_chunks(log_a), axis=3)
    causal = np.tril(np.ones((CHUNK, CHUNK), dtype=bool))[:, :, None]

    def step(state, inp):
        q_, k_, v_, b_ = inp
        diff = b_[:, :, :, None, :] - b_[:, :, None, :, :]
        decay = jnp.exp(jnp.where(causal, diff, -jnp.inf))
        scores = jnp.einsum('bhtd,bhsd,bhtsd->bhts', q_, k_, decay)
        o = jnp.einsum('bhts,bhse->bhte', scores, v_) + jnp.einsum('bhtd,bhde->bhte', q_ * jnp.exp(b_), state)
        b_last = b_[:, :, -1:, :]
        state = jnp.exp(b_last[:, :, 0, :])[..., None] * state + jnp.einsum('bhsd,bhse->bhde', k_ * jnp.exp(b_last - b_), v_)
        return state, o

    s0 = jnp.zeros((b, h, dk, dv), jnp.float32)
    _, o = lax.scan(step, s0, (qc, kc, vc, bc))
    return o.transpose(1, 0, 3, 2, 4).reshape(b, s, h, dv).astype(out_dtype)


def hgrn2_mixer(q_raw, f_raw, i_raw, g_raw, lb, norm_g):
    b, s, _ = q_raw.shape
    q = jax.nn.silu(q_raw).reshape(b, s, HGRN_HEADS, HGRN_EXPAND)
    f = lb + (1.0 - lb) * jax.nn.sigmoid(f_raw.astype(jnp.float32))
    log_f = jnp.log(f).reshape(b, s, HGRN_HEADS, HGRN_EXPAND)
    k = (1.0 - f).reshape(b, s, HGRN_HEADS, HGRN_EXPAND)
    i = i_raw.reshape(b, s, HGRN_HEADS, HGRN_DV)
    o = chunk_gated_recurrence(q, k, i, log_f)
    o = rmsnorm(o, norm_g.reshape(HGRN_HEADS, HGRN_DV)).reshape(b, s, HGRN_DV_TOTAL)
    return o * jax.nn.silu(g_raw)


def gla_mixer(q_raw, k_raw, v_raw, r_raw, a_lr, wa2, ba, norm_g):
    b, s, _ = q_raw.shape
    q = q_raw.reshape(b, s, GLA_HEADS, GLA_DK) * (GLA_DK ** -0.5)
    k = k_raw.reshape(b, s, GLA_HEADS, GLA_DK)
    v = v_raw.reshape(b, s, GLA_HEADS, GLA_DV)
    log_a = jax.nn.log_sigmoid((a_lr @ wa2 + ba).astype(jnp.float32)) / GLA_GATE_NORMALIZER
    o = chunk_gated_recurrence(q, k, v, log_a.reshape(b, s, GLA_HEADS, GLA_DK))
    o = rmsnorm(o, norm_g.reshape(GLA_HEADS, GLA_DV)).reshape(b, s, GLA_DV_TOTAL)
    return o * jax.nn.silu(r_raw)


def setup_inputs(seed: int = 0) -> dict:
    key = jax.random.key(seed)
    ks = jax.random.split(key, 16)
    f32 = jnp.float32
    nrm = lambda k, shape, sc: jax.random.normal(k, shape, f32) * sc
    return {
        "x": nrm(ks[0], (BATCH, SEQ, D_MODEL), 1.0),
        "norm_g": 1.0 + nrm(ks[1], (DEPTH, 4, D_MODEL), 0.02),
        "even_w_in": nrm(ks[2], (N_EVEN, D_MODEL, EVEN_IN), D_MODEL ** -0.5),
        "even_conv_w": nrm(ks[3], (N_EVEN, CONV_WIDTH, CONV_DIM), CONV_WIDTH ** -0.5),
        "even_rel_bias": nrm(ks[4], (N_EVEN, ATT_HEADS, REL_TABLE), 0.1),
        "even_w_out": nrm(ks[5], (N_EVEN, D_MODEL, D_MODEL), D_MODEL ** -0.5),
        "odd_w_in": nrm(ks[6], (N_ODD, D_MODEL, ODD_IN), D_MODEL ** -0.5),
        "hgrn_lb": nrm(ks[7], (DEPTH, HGRN_DK_TOTAL), 0.5),
        "hgrn_norm_g": 1.0 + nrm(ks[8], (N_ODD, HGRN_DV_TOTAL), 0.02),
        "gla_wa2": nrm(ks[9], (N_ODD, GLA_GATE_RANK, GLA_DK_TOTAL), GLA_GATE_RANK ** -0.5),
        "gla_ba": nrm(ks[10], (N_ODD, GLA_DK_TOTAL), 0.01),
        "gla_norm_g": 1.0 + nrm(ks[11], (N_ODD, GLA_DV_TOTAL), 0.02),
        "odd_w_out": nrm(ks[12], (N_ODD, D_MODEL, D_MODEL), D_MODEL ** -0.5),
        "mlp_w1": nrm(ks[13], (DEPTH, D_MODEL, D_FF), D_MODEL ** -0.5),
        "mlp_w2": nrm(ks[14], (DEPTH, D_FF, D_MODEL), D_FF ** -0.5),
    }


def reference(x, norm_g, even_w_in, even_conv_w, even_rel_bias, even_w_out,
              odd_w_in, hgrn_lb, hgrn_norm_g, gla_wa2, gla_ba, gla_norm_g, odd_w_out,
              mlp_w1, mlp_w2):
    b, s, _ = x.shape
    lb_soft = jax.nn.softmax(hgrn_lb.astype(jnp.float32), axis=0)
    lb_all = jnp.cumsum(lb_soft, axis=0) - lb_soft[0]
    h = x
    for l in range(DEPTH):
        g = norm_g[l]
        u = rmsnorm(h, g[0])
        if l % 2 == 0:
            e = l // 2
            proj = u @ even_w_in[e]
            b_gate, c_gate, hc, q, k, v = split_cols(proj, EVEN_SIZES)
            ya = short_conv_mixer(b_gate, c_gate, hc, even_conv_w[e])
            shp = (b, s, ATT_HEADS, ATT_HEAD_DIM)
            yb = chunked_band_attention(q.reshape(shp), k.reshape(shp), v.reshape(shp), even_rel_bias[e])
            y = jnp.concatenate([ya, yb], axis=-1) @ even_w_out[e]
        else:
            o_i = l // 2
            proj = u @ odd_w_in[o_i]
            hq, hf, hi, hg, gq, gk, gv, gr, ga = split_cols(proj, ODD_SIZES)
            yc = hgrn2_mixer(hq, hf, hi, hg, lb_all[l], hgrn_norm_g[o_i])
            yd = gla_mixer(gq, gk, gv, gr, ga, gla_wa2[o_i], gla_ba[o_i], gla_norm_g[o_i])
            y = jnp.concatenate([yc, yd], axis=-1) @ odd_w_out[o_i]
        h = h + rmsnorm(y, g[1])
        u = rmsnorm(h, g[2])
        z = jnp.square(jax.nn.relu(u @ mlp_w1[l])) @ mlp_w2[l]
        h = h + rmsnorm(z, g[3])
    return h
```

```python
import contextlib
import numpy as np
import concourse.bass as bass
import concourse.mybir as mybir
from concourse.bass_utils import run_bass_kernel_spmd

F32 = mybir.dt.float32
BF16 = mybir.dt.bfloat16
ALU = mybir.AluOpType
AF = mybir.ActivationFunctionType

N_DMA_SEMS = 40
import os as _os0
STRICT_SAME_ENGINE = bool(int(_os0.environ.get("STRICT_SE", "1")))
D = 2048
KC = 16
DFF = 8192
HALO = 512
EPS = 1e-6
NEG = -30000.0


class Buf:
    __slots__ = ("w", "rs", "excl")

    def __init__(self, excl=False):
        self.w = None
        self.rs = []
        self.excl = excl


class Op:
    __slots__ = ("eng", "fn", "seq", "waits", "sig", "tick", "dma", "dsem", "dval", "vc", "dwaits")


class Sched:
    ENGS = ("pe", "act", "dve", "pool", "sp")

    def __init__(self, nc):
        self.nc = nc
        self.ops = {e: [] for e in self.ENGS}
        self.known = {e: {f: -1 for f in self.ENGS} for e in self.ENGS}
        self.known_dma = {e: {} for e in self.ENGS}
        self.dma_last = [None] * N_DMA_SEMS
        self.dma_cnt = [0] * N_DMA_SEMS
        self.dma_pool = {"sp": list(range(0, 24)), "pool": list(range(24, N_DMA_SEMS)), "act": []}
        self.dma_rr = {"sp": 0, "pool": 0, "act": 0}

    def _add(self, eng, fn, reads, writes, dma):
        o = Op()
        o.eng = eng
        o.fn = fn
        o.dma = dma
        o.sig = False
        o.tick = 0
        o.waits = []
        o.dwaits = []
        o.seq = len(self.ops[eng])
        kn = self.known[eng]
        kd = self.known_dma[eng]
        deps = []
        for r in reads:
            if r.w is not None:
                deps.append((r.w, True))
            if r.excl:
                for rr in r.rs:
                    if rr.eng != eng:
                        deps.append((rr, False))
        for w in writes:
            if w.w is not None:
                deps.append((w.w, False))
            for rr in w.rs:
                deps.append((rr, False))
        for d, raw in deps:
            if d.dma:
                if kd.get(d.dsem, 0) < d.dval:
                    kd[d.dsem] = d.dval
                    o.dwaits.append((d.dsem, d.dval))
            else:
                if d.eng == eng and not dma:
                    if eng == "pe" or (not raw and not STRICT_SAME_ENGINE):
                        continue
                if kn[d.eng] >= d.seq:
                    continue
                kn[d.eng] = d.seq
                for f, s in d.vc.items():
                    if f != eng and kn[f] < s:
                        kn[f] = s
                d.sig = True
                o.waits.append(d)
        if dma:
            pool_ = self.dma_pool[eng]
            k = pool_[self.dma_rr[eng] % len(pool_)]
            self.dma_rr[eng] += 1
            prev = self.dma_last[k]
            if prev is not None and kd.get(k, 0) < prev.dval:
                kd[k] = prev.dval
                o.dwaits.append((k, prev.dval))
            self.dma_cnt[k] += 16
            o.dsem = k
            o.dval = self.dma_cnt[k]
            self.dma_last[k] = o
        best = {}
        for d in o.waits:
            if d.eng not in best or best[d.eng].seq < d.seq:
                best[d.eng] = d
        o.waits = list(best.values())
        o.vc = dict(kn)
        o.vc[eng] = o.seq - 1 if dma else o.seq
        for r in reads:
            r.rs.append(o)
        for w in writes:
            w.w = o
            w.rs = []
        self.ops[eng].append(o)
        return o

    def op(self, eng, fn, reads=(), writes=()):
        return self._add(eng, fn, reads, writes, False)

    def dma(self, q, out, in_, reads=(), writes=()):
        return self._add(q, lambda e: e.dma_start(out=out, in_=in_), reads, writes, True)

    def emit(self):
        nc = self.nc
        with contextlib.ExitStack() as st:
            esem = {e: st.enter_context(nc.semaphore("s_" + e)) for e in self.ENGS}
            dsem = [st.enter_context(nc.semaphore("d%d" % i)) for i in range(N_DMA_SEMS)]
            for e in self.ENGS:
                t = 0
                for o in self.ops[e]:
                    if o.sig and not o.dma:
                        t += 1
                        o.tick = t
            block = st.enter_context(nc.Block())

            def run(ename):
                def body(eng):
                    s_own = esem[ename]
                    for o in self.ops[ename]:
                        for d in o.waits:
                            eng.wait_ge(esem[d.eng], d.tick)
                        for (k, v) in o.dwaits:
                            eng.wait_ge(dsem[k], v)
                        ins = o.fn(eng)
                        if o.dma:
                            ins.then_inc(dsem[o.dsem], 16)
                        elif o.sig:
                            ins.then_inc(s_own, 1)
                return body

            block.tensor(run("pe"))
            block.scalar(run("act"))
            block.vector(run("dve"))
            block.gpsimd(run("pool"))
            block.sync(run("sp"))


class Arena:
    def __init__(self, nc, lo, hi):
        self.nc = nc
        self.lo = lo
        self.hi = hi
        self.top = lo
        self.dead = []
        self.live = []
        self.n = 0

    def alloc(self, shape, dtype, nbuf=1):
        nb = 4 if dtype == F32 else 2
        size = nb
        for s in shape[1:]:
            size *= s
        size = (size + 63) // 64 * 64
        off = self.top
        self.top += size
        assert self.top <= self.hi, ("SBUF arena overflow", self.top, self.hi)
        self.n += 1
        t = self.nc.alloc_sbuf_tensor_at("t%d" % self.n, list(shape), dtype, offset=off)
        bufs = [Buf() for _ in range(nbuf)]
        acc = []
        keep = []
        for (lo, hi, bl) in self.dead:
            if lo < off + size and off < hi:
                for b in bl:
                    if b.w is not None:
                        acc.append(b.w)
                    acc.extend(b.rs)
                if not (off <= lo and hi <= off + size):
                    keep.append((lo, hi, bl))
            else:
                keep.append((lo, hi, bl))
        self.dead = keep
        for b in bufs:
            b.rs = list(acc)
        self.live.append((off, off + size, bufs))
        return t, bufs

    def mark(self):
        return (self.top, len(self.live))

    def release(self, m):
        top, n = m
        self.dead.extend(self.live[n:])
        del self.live[n:]
        self.top = top


def build(TOKC, NL=2, dbg=False):
    nc = bass.Bass("TRN2", target_bir_lowering=False)
    NTT = TOKC // 512
    HT = TOKC // 2
    HTT = HT // 512
    TM = min(1024, TOKC)
    NTOT = HALO + TOKC

    def din(name, shape, dt=F32):
        return nc.dram_tensor(name, list(shape), dt, kind="ExternalInput").ap()

    x = din("x", [TOKC, D])
    w_in0 = din("w_in0", [D, 6144])
    w_out0 = din("w_out0", [D, D])
    w_in1 = din("w_in1", [D, 7184])
    w_out1 = din("w_out1", [D, D])
    w1 = [din("w1_%d" % l, [D, DFF]) for l in range(2)]
    w2 = [din("w2_%d" % l, [DFF, D]) for l in range(2)]
    gT_d = din("gT", [128, 128])
    convT_d = din("convT", [128, 24])
    biasT_d = din("biasT", [8, 128, 640])
    cst_d = din("cst", [128, 3 * 128])
    lbT_d = din("lbT", [128, 16])
    lbrow_d = din("lbrow", [128, 2048])
    hngT_d = din("hngT", [128, 8])
    gngT_d = din("gngT", [128, 8])
    wa2_d = din("wa2", [16, 512])
    barow_d = din("barow", [128, 512])
    out = nc.dram_tensor("out", [TOKC, D], F32, kind="ExternalOutput").ap()

    hT = nc.dram_tensor("hT", [D, TOKC], F32).ap()
    uT = nc.dram_tensor("uT", [D, NTOT], BF16).ap()
    yT = nc.dram_tensor("yT", [D, TOKC], BF16).ap()
    zT = nc.dram_tensor("zT", [D, TM], F32).ap()
    hTb = {(c, t): Buf() for c in range(KC) for t in range(NTT)}
    uTb = {(c, t): Buf() for c in range(KC) for t in range(NTT + 1)}
    yTb = {(c, t): Buf() for c in range(KC) for t in range(NTT)}
    zTb = {(c, t): Buf() for c in range(KC) for t in range(TM // 512)}
    outbs = []
    dbgs = {}

    S = Sched(nc)
    A = Arena(nc, 16640, 229376)

    def MM(o, l, r, start, stop, reads, writes):
        S.op("pe", lambda e: e.matmul(o, l, r, start=start, stop=stop), reads, writes)

    def TR(o, i, ident, reads, writes):
        S.op("pe", lambda e: e.transpose(o, i, ident), reads, writes)

    def ACT(o, i, func, reads, writes, bias=None, scale=None):
        kw = {}
        if bias is not None:
            kw["bias"] = bias
        if scale is not None:
            kw["scale"] = scale
        S.op("act", lambda e: e.activation(o, i, func, **kw), reads, writes)

    def TT(eng, o, a, b, op, reads, writes):
        S.op(eng, lambda e: e.tensor_tensor(o, a, b, op), reads, writes)

    def TS(eng, o, a, s1, s2, op0, op1, reads, writes):
        if op1 is None:
            S.op(eng, lambda e: e.tensor_scalar(o, a, s1, None, op0), reads, writes)
        else:
            S.op(eng, lambda e: e.tensor_scalar(o, a, s1, s2, op0, op1), reads, writes)

    def STT(eng, o, a, s, b, op0, op1, reads, writes):
        assert eng == "dve"
        S.op(eng, lambda e: e.scalar_tensor_tensor(o, a, s, b, op0, op1), reads, writes)

    def CP(eng, o, i, reads, writes):
        if eng == "act":
            S.op("act", lambda e: e.copy(o, i), reads, writes)
        else:
            S.op(eng, lambda e: e.tensor_copy(o, i), reads, writes)

    def RCP(o, i, reads, writes):
        S.op("dve", lambda e: e.reciprocal(o, i), reads, writes)

    def MSET(eng, o, v, writes):
        S.op(eng, lambda e: e.memset(o, v), (), writes)

    ps = []
    psb = []
    for i in range(8):
        ps.append(nc.alloc_psum_tensor("ps%d" % i, [128, 512], F32))
        psb.append(Buf(excl=True))

    cst, cstb = A.alloc([128, 384], F32)
    cstb = cstb[0]
    S.dma("sp", cst[:], cst_d[:, :], writes=[cstb])
    ident = cst[:, 0:128]
    U2 = cst[:, 128:256]
    Rm = cst[:, 256:384]
    ones, onesb = A.alloc([128, 128], BF16)
    onesb = onesb[0]
    MSET("dve", ones[:], 1.0, [onesb])
    gT, gTb = A.alloc([128, 128], F32)
    gTb = gTb[0]
    S.dma("sp", gT[:], gT_d[:, :], writes=[gTb])
    convT, convTb = A.alloc([128, 24], F32)
    convTb = convTb[0]
    S.dma("sp", convT[:], convT_d[:, :], writes=[convTb])
    zbf, zbfb = A.alloc([128, 512], BF16)
    zbfb = zbfb[0]
    MSET("dve", zbf[:], 0.0, [zbfb])
    negc, negcb = A.alloc([128, 1], F32)
    negcb = negcb[0]
    MSET("dve", negc[:], NEG, [negcb])

    for c in range(KC):
        S.dma("sp", uT[c * 128:(c + 1) * 128, 0:HALO], zbf[:, 0:HALO], reads=[zbfb], writes=[uTb[(c, 0)]])

    class WPool:
        def __init__(self, n, elems=4096):
            self.t = []
            self.prev = []
            for i in range(n):
                t, b = A.alloc([128, elems], BF16)
                self.t.append(t)
                self.prev.append(b)
            self.i = 0

        def load_multi(self, blocks, K):
            i = self.i
            self.i = (i + 1) % len(self.t)
            tot = sum(n for _, n in blocks)
            v = self.t[i][:, 0:K * tot].rearrange("p (k n) -> p k n", k=K)
            haz = []
            for b in self.prev[i]:
                if b.w is not None:
                    haz.append(b.w)
                haz.extend(b.rs)
            bufs = []
            c0 = 0
            for src, n in blocks:
                b = Buf()
                b.rs = list(haz)
                S.dma("pool", v[:, :, c0:c0 + n], src.rearrange("(k p) n -> p k n", p=128), writes=[b])
                bufs.append(b)
                c0 += n
            self.prev[i][:] = bufs
            return v, bufs

        def load(self, src_ap, K, ncols):
            v, bufs = self.load_multi([(src_ap, ncols)], K)
            return v, bufs[0]

    eng_rr = [0]

    def alt(*engs):
        eng_rr[0] += 1
        return engs[eng_rr[0] % len(engs)]

    UQ = ["pool"]

    def norm_sq(ht, htb, c, sq, sqb):
        s = c % len(sq)
        ACT(sq[s][:], ht[c], AF.Square, [htb[c]], [sqb[s]])
        MM(ps[7][:], ones[:], sq[s][:], c == 0, c == KC - 1, [onesb, sqb[s]], [psb[7]])

    def norm_to_u(ht, htb, gcol, tt_u, sq, sqb, ub, ubb, rstd, rstdb, presq=False):
        ssb = 7
        if not presq:
            for c in range(KC):
                norm_sq(ht, htb, c, sq, sqb)
        ACT(rstd[:], ps[ssb][:], AF.Ln, [psb[ssb]], [rstdb], bias=EPS, scale=1.0 / D)
        ACT(rstd[:], rstd[:], AF.Exp, [rstdb], [rstdb], scale=-0.5)
        for c in range(KC):
            s = c % len(ub)
            STT("dve", ub[s][:], ht[c], gT[:, gcol + c:gcol + c + 1], rstd[:], ALU.mult, ALU.mult,
                [htb[c], gTb, rstdb], [ubb[s]])
            S.dma(UQ[0], uT[c * 128:(c + 1) * 128, tt_u * 512:(tt_u + 1) * 512], ub[s][:],
                  reads=[ubb[s]], writes=[uTb[(c, tt_u)]])

    def ep_head(tt, ot, otb, ssbank, gpost, gnext, hin, hinb, sq, sqb, ub, ubb, rstd, rstdb, osb=None, osbb=None):
        ACT(rstd[:], ps[ssbank][:], AF.Ln, [psb[ssbank]], [rstdb], bias=EPS, scale=1.0 / D)
        ACT(rstd[:], rstd[:], AF.Exp, [rstdb], [rstdb], scale=-0.5)

    def ep_chunk(c, tt, ot, otb, ssbank, gpost, gnext, hin, hinb, sq, sqb, ub, ubb, rstd, rstdb, osb=None, osbb=None):
        s = c % len(hin)
        S.dma("sp", hin[s][:], hT[c * 128:(c + 1) * 128, tt * 512:(tt + 1) * 512],
              reads=[hTb[(c, tt)]], writes=[hinb[s]])
        STT("dve", ot[c], ot[c], gT[:, gpost + c:gpost + c + 1], rstd[:], ALU.mult, ALU.mult,
            [otb[c], gTb, rstdb], [otb[c]])
        TT("dve", ot[c], ot[c], hin[s][:], ALU.add, [otb[c], hinb[s]], [otb[c]])
        if gnext is not None:
            norm_sq(ot, otb, c, sq, sqb)

    def epilogue(tt, ot, otb, ssbank, gpost, gnext, hin, hinb, sq, sqb, ub, ubb, rstd, rstdb, osb=None, osbb=None):
        args = (tt, ot, otb, ssbank, gpost, gnext, hin, hinb, sq, sqb, ub, ubb, rstd, rstdb, osb, osbb)
        ep_head(*args)
        for c in range(KC):
            ep_chunk(c, *args)
        ep_tail(*args)

    def ep_tail(tt, ot, otb, ssbank, gpost, gnext, hin, hinb, sq, sqb, ub, ubb, rstd, rstdb, osb=None, osbb=None):
        if gnext is not None:
            for c in range(KC):
                S.dma("sp", hT[c * 128:(c + 1) * 128, tt * 512:(tt + 1) * 512], ot[c],
                      reads=[otb[c]], writes=[hTb[(c, tt)]])
            norm_to_u(ot, otb, gnext, tt + 1, sq, sqb, ub, ubb, rstd, rstdb, presq=True)
        else:
            for tb in range(4):
                r0 = tt * 512 + tb * 128
                for ch in range(2):
                    s = ch
                    for c2 in range(2):
                        c4 = ch * 2 + c2
                        bank = c4 % 2
                        for cc in range(4):
                            c = c4 * 4 + cc
                            TR(ps[bank][:, cc * 128:(cc + 1) * 128], ot[c][:, tb * 128:(tb + 1) * 128], ident,
                               [otb[c], cstb], [psb[bank]])
                        CP(alt("act", "dve"), osb[s][:, c2 * 512:(c2 + 1) * 512], ps[bank][:], [psb[bank]], [osbb[s]])
                    ob_ = Buf()
                    outbs.append(ob_)
                    S.dma("sp", out[r0:r0 + 128, ch * 1024:(ch + 1) * 1024], osb[s][:], reads=[osbb[s]], writes=[ob_])

    def stage0():
        m = A.mark()
        xs = []
        xsb = []
        for i in range(2):
            t, b = A.alloc([128, 4, D], F32, nbuf=4)
            xs.append(t)
            xsb.append(b)
        ht = []
        htb = []
        for c in range(KC):
            t, b = A.alloc([128, 512], F32)
            ht.append(t[:])
            htb.append(b[0])
        sq, sqb, ub, ubb = [], [], [], []
        for i in range(3):
            t, b = A.alloc([128, 512], BF16)
            sq.append(t)
            sqb.append(b[0])
            t, b = A.alloc([128, 512], BF16)
            ub.append(t)
            ubb.append(b[0])
        rstd, rstdb = A.alloc([128, 512], F32)
        rstdb = rstdb[0]
        for tt in range(NTT):
            s = tt % 2
            for tb in range(4):
                r0 = (tt * 4 + tb) * 128
                S.dma("sp", xs[s][:, tb, :], x[r0:r0 + 128, :], writes=[xsb[s][tb]])
            for c in range(KC):
                bank = c % 4
                for tb in range(4):
                    TR(ps[bank][:, tb * 128:(tb + 1) * 128], xs[s][:, tb, c * 128:(c + 1) * 128], ident,
                       [xsb[s][tb], cstb], [psb[bank]])
                CP(alt("act", "dve"), ht[c], ps[bank][:], [psb[bank]], [htb[c]])
                S.dma("pool", hT[c * 128:(c + 1) * 128, tt * 512:(tt + 1) * 512], ht[c],
                      reads=[htb[c]], writes=[hTb[(c, tt)]])
            norm_to_u(ht, htb, 0, tt + 1, sq, sqb, ub, ubb, rstd, rstdb)
        A.release(m)

    def mixer0(half):
        m = A.mark()
        t0 = half * HT
        NH = HALO + HT
        u, ub_ = A.alloc([128, KC, NH], BF16, nbuf=KC)
        for c in range(KC):
            rd = [uTb[(c, t)] for t in range(t0 // 512, t0 // 512 + NH // 512)]
            S.dma("sp", u[:, c, :], uT[c * 128:(c + 1) * 128, t0:t0 + NH], reads=rd, writes=[ub_[c]])
        WP = WPool(4)
        ybf = []
        ybfb = []
        for i in range(2):
            t, b = A.alloc([128, HT], BF16)
            ybf.append(t)
            ybfb.append(b[0])
        yi = [0]

        def proj_fm(wv, wb, col, tok0, ntok, bank):
            for k in range(KC):
                MM(ps[bank][:, 0:ntok], wv[:, k, col:col + 128], u[:, k, tok0:tok0 + ntok], k == 0, k == KC - 1,
                   [wb, ub_[k]], [psb[bank]])

        mA = A.mark()
        NCV = HT + 128
        ct, ctb = A.alloc([128, NCV], F32)
        ctb = ctb[0]
        ucv, ucvb = A.alloc([128, NCV], F32)
        ucvb = ucvb[0]
        acc, accb = A.alloc([128, HT], F32)
        accb = accb[0]
        for gp in range(4):
            wbv, wbb = WP.load(w_in0[:, gp * 256:gp * 256 + 256], KC, 256)
            wcv, wcb = WP.load(w_in0[:, 1024 + gp * 256:1024 + gp * 256 + 256], KC, 256)
            whv, whb = WP.load(w_in0[:, 2048 + gp * 256:2048 + gp * 256 + 256], KC, 256)
            for g2 in range(2):
                g = gp * 2 + g2
                col = g2 * 128
                pieces = [(HALO - 128, 128)] + [(HALO + i * 512, 512) for i in range(HTT)]
                for pi, (tk, n) in enumerate(pieces):
                    o0 = tk - (HALO - 128)
                    bk = pi % 2
                    proj_fm(wcv, wcb, col, tk, n, bk)
                    CP("act", ct[:, o0:o0 + n], ps[bk][:, 0:n], [psb[bk]], [ctb])
                    bk2 = 2 + pi % 2
                    proj_fm(whv, whb, col, tk, n, bk2)
                    TT("dve", ucv[:, o0:o0 + n], ps[bk2][:, 0:n], ct[:, o0:o0 + n], ALU.mult, [psb[bk2], ctb], [ucvb])
                TS("dve", acc[:], ucv[:, 126:126 + HT], convT[:, g:g + 1], None, ALU.mult, None, [ucvb, convTb], [accb])
                STT("dve", acc[:], ucv[:, 127:127 + HT], convT[:, 8 + g:9 + g], acc[:], ALU.mult, ALU.add,
                    [ucvb, convTb, accb], [accb])
                STT("dve", acc[:], ucv[:, 128:128 + HT], convT[:, 16 + g:17 + g], acc[:], ALU.mult, ALU.add,
                    [ucvb, convTb, accb], [accb])
                ys = yi[0] % 2
                yi[0] += 1
                for i in range(HTT):
                    bk = 4 + i % 2
                    proj_fm(wbv, wbb, col, HALO + i * 512, 512, bk)
                    TT("dve", ybf[ys][:, i * 512:(i + 1) * 512], ps[bk][:], acc[:, i * 512:(i + 1) * 512], ALU.mult,
                       [psb[bk], accb], [ybfb[ys]])
                S.dma("sp", yT[g * 128:(g + 1) * 128, t0:t0 + HT], ybf[ys][:], reads=[ybfb[ys]],
                      writes=[yTb[(g, t)] for t in range(t0 // 512, t0 // 512 + HTT)])
        A.release(mA)

        NKB = NH // 128
        qT, qTb = A.alloc([128, HT], BF16)
        qTb = qTb[0]
        kT, kTb = A.alloc([128, NH], BF16)
        kTb = kTb[0]
        vt, vtb = A.alloc([128, NKB, 256], BF16)
        vtb = vtb[0]
        bT = []
        bTb = []
        for i in range(2):
            t, b = A.alloc([128, 640], F32)
            bT.append(t)
            bTb.append(b[0])
        et, etb, pT, pTb, rd_, rdb = [], [], [], [], [], []
        for i in range(2):
            t, b = A.alloc([128, 640], F32)
            et.append(t)
            etb.append(b[0])
            t, b = A.alloc([128, 640], BF16)
            pT.append(t)
            pTb.append(b[0])
            t, b = A.alloc([128, 128], F32)
            rd_.append(t)
            rdb.append(b[0])
        scale = 128.0 ** -0.5
        for hp in range(4):
            wqv, wqb = WP.load(w_in0[:, 3072 + hp * 256:3072 + hp * 256 + 256], KC, 256)
            wkv, wkb = WP.load(w_in0[:, 4096 + hp * 256:4096 + hp * 256 + 256], KC, 256)
            wvv, wvb = WP.load(w_in0[:, 5120 + hp * 256:5120 + hp * 256 + 256], KC, 256)
            for tb in range(NKB):
                bk = tb % 2
                for k in range(KC):
                    MM(ps[bk][:, 0:256], u[:, k, tb * 128:(tb + 1) * 128], wvv[:, k, :], k == 0, k == KC - 1,
                       [ub_[k], wvb], [psb[bk]])
                CP(alt("act", "dve"), vt[:, tb, :], ps[bk][:, 0:256], [psb[bk]], [vtb])
            for h2 in range(2):
                hd = hp * 2 + h2
                col = h2 * 128
                bs = hd % 2
                S.dma("sp", bT[bs][:], biasT_d[hd], writes=[bTb[bs]])
                for i in range(HTT):
                    bk = 2 + i % 2
                    proj_fm(wqv, wqb, col, HALO + i * 512, 512, bk)
                    ACT(qT[:, i * 512:(i + 1) * 512], ps[bk][:], AF.Copy, [psb[bk]], [qTb], scale=scale)
                for i in range(NH // 512):
                    bk = 2 + i % 2
                    proj_fm(wkv, wkb, col, i * 512, 512, bk)
                    CP(alt("act", "dve"), kT[:, i * 512:(i + 1) * 512], ps[bk][:], [psb[bk]], [kTb])
                ys = yi[0] % 2
                yi[0] += 1
                for qt in range(HT // 128):
                    s = qt % 2
                    bx = 4 - 2 * (qt % 2)
                    bo = 6 + qt % 2
                    for kb in range(5):
                        dst = ps[bx][:, kb * 128:(kb + 1) * 128] if kb < 4 else ps[bx + 1][:, 0:128]
                        dbuf = psb[bx] if kb < 4 else psb[bx + 1]
                        MM(dst, kT[:, (qt + kb) * 128:(qt + kb + 1) * 128], qT[:, qt * 128:(qt + 1) * 128], True, True,
                           [kTb, qTb], [dbuf])
                    TT("dve", et[s][:, 0:512], ps[bx][:], bT[bs][:, 0:512], ALU.add, [psb[bx], bTb[bs]], [etb[s]])
                    TT("dve", et[s][:, 512:640], ps[bx + 1][:, 0:128], bT[bs][:, 512:640], ALU.add,
                       [psb[bx + 1], bTb[bs]], [etb[s]])
                    nh = max(0, 4 - qt) if half == 0 else 0
                    if nh > 0:
                        ACT(pT[s][:, 0:nh * 128], et[s][:, 0:nh * 128], AF.Exp, [etb[s], negcb], [pTb[s]], bias=negc[:])
                    ACT(pT[s][:, nh * 128:640], et[s][:, nh * 128:640], AF.Exp, [etb[s]], [pTb[s]])
                    for kb in range(5):
                        MM(ps[bo][:, 0:128], vt[:, qt + kb, col:col + 128], pT[s][:, kb * 128:(kb + 1) * 128], kb == 0, kb == 4,
                           [vtb, pTb[s]], [psb[bo]])
                    for kb in range(5):
                        MM(ps[bo][:, 128:256], ones[:], pT[s][:, kb * 128:(kb + 1) * 128], kb == 0, kb == 4,
                           [onesb, pTb[s]], [psb[bo]])
                    RCP(rd_[s][:], ps[bo][:, 128:256], [psb[bo]], [rdb[s]])
                    TT("dve", ybf[ys][:, qt * 128:(qt + 1) * 128], ps[bo][:, 0:128], rd_[s][:], ALU.mult,
                       [psb[bo], rdb[s]], [ybfb[ys]])
                c = 8 + hd
                S.dma("sp", yT[c * 128:(c + 1) * 128, t0:t0 + HT], ybf[ys][:], reads=[ybfb[ys]],
                      writes=[yTb[(c, t)] for t in range(t0 // 512, t0 // 512 + HTT)])
        A.release(m)


    stS = nc.dram_tensor("stS", [128, 12 * 256], F32).ap()
    stSb = [Buf() for _ in range(12)]

    def ring(shape, dtype, n=2):
        ts, bs = [], []
        for i in range(n):
            t, b = A.alloc(shape, dtype)
            ts.append(t)
            bs.append(b[0])
        return ts, bs

    def mixer1(half):
        m = A.mark()
        t0 = half * HT
        u, ub_ = A.alloc([128, KC, HT], BF16, nbuf=KC)
        for c in range(KC):
            S.dma("sp", u[:, c, :], uT[c * 128:(c + 1) * 128, HALO + t0:HALO + t0 + HT],
                  reads=[uTb[(c, 1 + t0 // 512 + t)] for t in range(HTT)], writes=[ub_[c]])
        WP = WPool(3, 8192)
        lbr, lbrb = A.alloc([128, 1024], F32)
        lbrb = lbrb[0]
        omlr, omlrb = A.alloc([128, 1024], F32)
        omlrb = omlrb[0]
        lbc, lbcb = A.alloc([128, 8], F32)
        lbcb = lbcb[0]
        omlc, omlcb = A.alloc([128, 8], F32)
        omlcb = omlcb[0]
        nomlc, nomlcb = A.alloc([128, 8], F32)
        nomlcb = nomlcb[0]
        mt = A.mark()
        lr, lrb = A.alloc([128, 2048], F32)
        lrb = lrb[0]
        lc, lcb = A.alloc([128, 16], F32)
        lcb = lcb[0]
        S.dma("sp", lr[:], lbrow_d[:, :], writes=[lrb])
        S.dma("sp", lc[:], lbT_d[:, :], writes=[lcb])
        TT("dve", lbr[:], lr[:, 1024:2048], lr[:, 0:1024], ALU.subtract, [lrb], [lbrb])
        ACT(lbr[:], lbr[:], AF.Exp, [lbrb], [lbrb], scale=-1.0)
        TS("pool", lbr[:], lbr[:], 1.0, None, ALU.add, None, [lbrb], [lbrb])
        RCP(lbr[:], lbr[:], [lbrb], [lbrb])
        TS("pool", omlr[:], lbr[:], -1.0, 1.0, ALU.mult, ALU.add, [lbrb], [omlrb])
        TT("dve", lbc[:], lc[:, 8:16], lc[:, 0:8], ALU.subtract, [lcb], [lbcb])
        ACT(lbc[:], lbc[:], AF.Exp, [lbcb], [lbcb], scale=-1.0)
        TS("pool", lbc[:], lbc[:], 1.0, None, ALU.add, None, [lbcb], [lbcb])
        RCP(lbc[:], lbc[:], [lbcb], [lbcb])
        TS("pool", omlc[:], lbc[:], -1.0, 1.0, ALU.mult, ALU.add, [lbcb], [omlcb])
        TS("pool", nomlc[:], lbc[:], -1.0, None, ALU.add, None, [lbcb], [nomlcb])
        A.release(mt)
        wa2s, wa2sb = A.alloc([16, 512], F32)
        wa2sb = wa2sb[0]
        S.dma("sp", wa2s[:], wa2_d[:, :], writes=[wa2sb])
        barow, barowb = A.alloc([128, 512], F32)
        barowb = barowb[0]
        S.dma("sp", barow[:], barow_d[:, :], writes=[barowb])
        hng, hngb = A.alloc([128, 8], F32)
        hngb = hngb[0]
        S.dma("sp", hng[:], hngT_d[:, :], writes=[hngb])
        gng, gngb = A.alloc([128, 8], F32)
        gngb = gngb[0]
        S.dma("sp", gng[:], gngT_d[:, :], writes=[gngb])
        gaT, gaTb = A.alloc([16, HT], F32)
        gaTb = gaTb[0]
        wga, wgab = WP.load(w_in1[:, 7168:7184], KC, 16)
        for g in range(HTT):
            for k in range(KC):
                MM(ps[7][0:16, :], wga[:, k, 0:16], u[:, k, g * 512:(g + 1) * 512], k == 0, k == KC - 1,
                   [wgab, ub_[k]], [psb[7]])
            CP("act", gaT[0:16, g * 512:(g + 1) * 512], ps[7][0:16, :], [psb[7]], [gaTb])
        qs, qsb = ring([128, 512], F32, 2)
        kf, kfb = ring([128, 512], F32, 2)
        sg4, sg4b = ring([128, 512], F32, 2)
        tA, tAb = ring([128, 512], F32, 2)
        gs0, gs0b = ring([128, 512], F32, 1)
        gs1, gs1b = ring([128, 512], F32, 1)
        oh0, oh0b = ring([128, 512], F32, 1)
        oh1, oh1b = ring([128, 512], F32, 1)
        sqr, sqrb = ring([128, 512], BF16)
        rs, rsb = ring([128, 512], F32, 1)
        yb0, yb0b = ring([128, 512], BF16)
        yb1, yb1b = ring([128, 512], BF16)
        X1, X1b = ring([128, 512], F32, 1)
        X2, X2b = ring([128, 512], F32, 1)
        la4, la4b = ring([128, 512], F32, 1)
        kt4, kt4b = ring([128, 512], F32, 2)
        kh4, kh4b = ring([128, 512], BF16, 1)
        ktl4, ktl4b = ring([128, 512], BF16, 1)
        vt4, vt4b = ring([128, 4, 256], BF16, 3)
        qtl4, qtl4b = ring([128, 512], BF16)
        AT4, AT4b = ring([128, 512], BF16)
        c4a, c4ab = ring([128, 512], F32, 1)
        c4b, c4bb = ring([128, 512], F32, 1)
        U4, U4b = ring([128, 512], F32, 1)
        for j in range(4):
            CP("act", U4[0][:, j * 128:(j + 1) * 128], U2, [cstb], [U4b[0]])
        S32, S32b = ring([128, 256], F32)
        Sring, Sringb = ring([128, 256], BF16, 18)
        qscale = 128.0 ** -0.5

        def silu_from_psum(bank, dst, dstb, ti=0):
            ACT(tA[ti][:], ps[bank][:], AF.Sigmoid, [psb[bank]], [tAb[ti]])
            TT("dve", dst, ps[bank][:], tA[ti][:], ALU.mult, [psb[bank], tAb[ti]], [dstb])

        heads = [("h", i) for i in range(8)] + [("g", i) for i in range(4)]
        import os as _os
        if _os.environ.get("DBG_HEADS"):
            heads = [h for h in heads if h[0] in _os.environ["DBG_HEADS"]][:int(_os.environ.get("DBG_NH", "12"))]
        scount = [0]
        for hidx, (kind, hi) in enumerate(heads):
            dv = 128 if kind == "h" else 256
            NE = dv // 128
            if kind == "h":
                tokb = [(w_in1[:, 1024 + hi * 128:1024 + hi * 128 + 128], 128),
                        (w_in1[:, 2048 + hi * 128:2048 + hi * 128 + 128], 128)]
                fmb = [(w_in1[:, hi * 128:hi * 128 + 128], 128),
                       (w_in1[:, 1024 + hi * 128:1024 + hi * 128 + 128], 128),
                       (w_in1[:, 3072 + hi * 128:3072 + hi * 128 + 128], 128)]
                yrow0 = hi * 128
                ng, ngb, ngc = hng, hngb, hi
                for j in range(4):
                    CP("act", c4a[0][:, j * 128:(j + 1) * 128], omlr[:, hi * 128:(hi + 1) * 128], [omlrb], [c4ab[0]])
                    CP("act", c4b[0][:, j * 128:(j + 1) * 128], lbr[:, hi * 128:(hi + 1) * 128], [lbrb], [c4bb[0]])
            else:
                tokb = [(w_in1[:, 4608 + hi * 128:4608 + hi * 128 + 128], 128),
                        (w_in1[:, 5120 + hi * 256:5120 + hi * 256 + 256], 256)]
                fmb = [(w_in1[:, 4096 + hi * 128:4096 + hi * 128 + 128], 128),
                       (w_in1[:, 4608 + hi * 128:4608 + hi * 128 + 128], 128),
                       (w_in1[:, 6144 + hi * 256:6144 + hi * 256 + 256], 256)]
                yrow0 = 1024 + hi * 256
                ng, ngb, ngc = gng, gngb, hi * 2
                for j in range(4):
                    CP("act", c4a[0][:, j * 128:(j + 1) * 128], barow[:, hi * 128:(hi + 1) * 128], [barowb], [c4ab[0]])
            wt, wtb = WP.load_multi(tokb, KC)
            wf, wfb = WP.load_multi(fmb, KC)
            st = {"sp": 0}
            sl0 = scount[0] % 18
            scount[0] += 1
            if half == 0:
                MSET("dve", S32[0][:, 0:dv], 0.0, [S32b[0]])
                MSET("pool", Sring[sl0][:, 0:dv], 0.0, [Sringb[sl0]])
            else:
                S.dma("sp", S32[0][:, 0:dv], stS[:, hidx * 256:hidx * 256 + dv], reads=[stSb[hidx]], writes=[S32b[0]])
                CP("act", Sring[sl0][:, 0:dv], S32[0][:, 0:dv], [S32b[0]], [Sringb[sl0]])
            cur = {"slot": sl0}

            def phaseA1(g):
                r = g % 2
                r3 = g % 3
                gsl = slice(g * 512, (g + 1) * 512)
                for k in range(KC):
                    MM(ps[0][:], wf[:, k, 0:128], u[:, k, gsl], k == 0, k == KC - 1, [wfb[0], ub_[k]], [psb[0]])
                if kind == "h":
                    silu_from_psum(0, qs[r][:], qsb[r])
                else:
                    ACT(qs[r][:], ps[0][:], AF.Copy, [psb[0]], [qsb[r]], scale=qscale)
                for k in range(KC):
                    MM(ps[1][:], wf[:, k, 128:256], u[:, k, gsl], k == 0, k == KC - 1, [wfb[1], ub_[k]], [psb[1]])
                if kind == "h":
                    ACT(tA[0][:], ps[1][:], AF.Sigmoid, [psb[1]], [tAb[0]])
                    TS("dve", kf[r][:], tA[0][:], nomlc[:, hi:hi + 1], omlc[:, hi:hi + 1], ALU.mult, ALU.add,
                       [tAb[0], nomlcb, omlcb], [kfb[r]])
                else:
                    CP("act", kf[r][:], ps[1][:], [psb[1]], [kfb[r]])
                for tl in range(4):
                    tok = (g * 4 + tl) * 128
                    bk = tl % 2
                    for k in range(KC):
                        MM(ps[bk][:, 0:128 + dv], u[:, k, tok:tok + 128], wt[:, k, :], k == 0, k == KC - 1,
                           [ub_[k], wtb[0], wtb[1]], [psb[bk]])
                    CP("act", vt4[r3][:, tl, 0:dv], ps[bk][:, 128:128 + dv], [psb[bk]], [vt4b[r3]])
                    if kind == "h":
                        ACT(sg4[r][:, tl * 128:(tl + 1) * 128], ps[bk][:, 0:128], AF.Sigmoid, [psb[bk]], [sg4b[r]])
                    else:
                        CP("dve", kt4[r][:, tl * 128:(tl + 1) * 128], ps[bk][:, 0:128], [psb[bk]], [kt4b[r]])

            def phaseA2(g):
                r = g % 2
                r3 = g % 3
                if kind == "h":
                    TT("pool", X2[0][:], sg4[r][:], c4a[0][:], ALU.mult, [sg4b[r], c4ab[0]], [X2b[0]])
                    TT("pool", X2[0][:], X2[0][:], c4b[0][:], ALU.add, [X2b[0], c4bb[0]], [X2b[0]])
                    ACT(la4[0][:], X2[0][:], AF.Ln, [X2b[0]], [la4b[0]])
                    TS("pool", kt4[r][:], X2[0][:], -1.0, 1.0, ALU.mult, ALU.add, [X2b[0]], [kt4b[r]])
                else:
                    for tl in range(4):
                        tok = (g * 4 + tl) * 128
                        MM(ps[2][:, tl * 128:(tl + 1) * 128], gaT[0:16, tok:tok + 128], wa2s[0:16, hi * 128:(hi + 1) * 128],
                           True, True, [gaTb, wa2sb], [psb[2]])
                    TT("dve", X1[0][:], ps[2][:], c4a[0][:], ALU.add, [psb[2], c4ab[0]], [X1b[0]])
                    ACT(X1[0][:], X1[0][:], AF.Exp, [X1b[0]], [X1b[0]], scale=-1.0)
                    ACT(X2[0][:], X1[0][:], AF.Ln, [X1b[0]], [X2b[0]], bias=1.0)
                    TS("pool", la4[0][:], X2[0][:], -1.0 / 16.0, None, ALU.mult, None, [X2b[0]], [la4b[0]])
                MM(ps[2][:], Rm, la4[0][:], True, True, [cstb, la4b[0]], [psb[2]])
                ACT(X1[0][:], ps[2][:], AF.Exp, [psb[2]], [X1b[0]])
                TT("pool", kh4[0][:], kt4[r][:], X1[0][:], ALU.mult, [kt4b[r], X1b[0]], [kh4b[0]])
                for tl in range(4):
                    MM(ps[3][:, tl * 128:(tl + 1) * 128], la4[0][:, tl * 128:(tl + 1) * 128], U2, True, True,
                       [la4b[0], cstb], [psb[3]])
                ACT(X2[0][:], ps[3][:], AF.Exp, [psb[3]], [X2b[0]])
                ACT(X1[0][:], ps[3][:], AF.Exp, [psb[3]], [X1b[0]], scale=-1.0)
                TT("dve", qtl4[r][:], qs[r][:], X2[0][:], ALU.mult, [qsb[r], X2b[0]], [qtl4b[r]])
                TT("pool", ktl4[0][:], kf[r][:], X1[0][:], ALU.mult, [kfb[r], X1b[0]], [ktl4b[0]])
                for tl in range(4):
                    MM(ps[4][:, tl * 128:(tl + 1) * 128], ktl4[0][:, tl * 128:(tl + 1) * 128], qtl4[r][:, tl * 128:(tl + 1) * 128],
                       True, True, [ktl4b[0], qtl4b[r]], [psb[4]])
                TT("dve", AT4[r][:], ps[4][:], U4[0][:], ALU.mult, [psb[4], U4b[0]], [AT4b[r]])
                slots = []
                for tl in range(4):
                    pc = (tl % 2) * dv
                    for ch in range(2):
                        MM(ps[6 + ch][:, pc:pc + dv], kh4[0][ch * 64:ch * 64 + 64, tl * 128:(tl + 1) * 128],
                           vt4[r3][ch * 64:ch * 64 + 64, tl, 0:dv], True, True, [kh4b[0], vt4b[r3]], [psb[6 + ch]])
                    for ch in range(2):
                        slots.append(cur["slot"])
                        sp = st["sp"]
                        col = tl * 128 + ch * 64 + 63
                        STT("dve", S32[1 - sp][:, 0:dv], S32[sp][:, 0:dv], X2[0][:, col:col + 1], ps[6 + ch][:, pc:pc + dv],
                            ALU.mult, ALU.add, [S32b[sp], X2b[0], psb[6 + ch]], [S32b[1 - sp]])
                        st["sp"] = 1 - sp
                        ns = scount[0] % 18
                        scount[0] += 1
                        CP("act", Sring[ns][:, 0:dv], S32[1 - sp][:, 0:dv], [S32b[1 - sp]], [Sringb[ns]])
                        cur["slot"] = ns
                return slots

            def phaseB(g, slots):
                r = g % 2
                r3 = g % 3
                gsl = slice(g * 512, (g + 1) * 512)
                gsv = [gs0[0], gs1[0]]
                gsvb = [gs0b[0], gs1b[0]]
                for eh in range(NE):
                    bk = eh % 2
                    for k in range(KC):
                        MM(ps[bk][:], wf[:, k, 256 + eh * 128:256 + (eh + 1) * 128], u[:, k, gsl], k == 0, k == KC - 1,
                           [wfb[2], ub_[k]], [psb[bk]])
                    silu_from_psum(bk, gsv[eh][:], gsvb[eh], 1)
                ohv = [oh0[0], oh1[0]]
                ohvb = [oh0b[0], oh1b[0]]
                obank = [5, 3]
                for eh in range(NE):
                    ob = obank[eh]
                    for tl in range(4):
                        MM(ps[ob][:, tl * 128:(tl + 1) * 128], vt4[r3][:, tl, eh * 128:(eh + 1) * 128], AT4[r][:, tl * 128:(tl + 1) * 128],
                           tl == 0, False, [vt4b[r3], AT4b[r]], [psb[ob]])
                    for tl in range(4):
                        for ch in range(2):
                            sl = slots[tl * 2 + ch]
                            c0 = tl * 128 + ch * 64
                            MM(ps[ob][:, c0:c0 + 64], Sring[sl][:, eh * 128:(eh + 1) * 128], qtl4[r][:, c0:c0 + 64],
                               False, tl == 3 and ch == 1, [Sringb[sl], qtl4b[r]], [psb[ob]])
                    CP("act", ohv[eh][:], ps[ob][:], [psb[ob]], [ohvb[eh]])
                for eh in range(NE):
                    ACT(sqr[eh][:], ohv[eh][:], AF.Square, [ohvb[eh]], [sqrb[eh]])
                    MM(ps[2][:], ones[:], sqr[eh][:], eh == 0, eh == NE - 1, [onesb, sqrb[eh]], [psb[2]])
                ACT(rs[0][:], ps[2][:], AF.Ln, [psb[2]], [rsb[0]], bias=EPS, scale=1.0 / dv)
                ACT(rs[0][:], rs[0][:], AF.Exp, [rsb[0]], [rsb[0]], scale=-0.5)
                yb = [yb0[r], yb1[r]]
                ybb = [yb0b[r], yb1b[r]]
                for eh in range(NE):
                    STT("dve", ohv[eh][:], ohv[eh][:], ng[:, ngc + eh:ngc + eh + 1], rs[0][:], ALU.mult, ALU.mult,
                        [ohvb[eh], ngb, rsb[0]], [ohvb[eh]])
                    TT("pool", yb[eh][:], ohv[eh][:], gsv[eh][:], ALU.mult, [ohvb[eh], gsvb[eh]], [ybb[eh]])
                    r0 = yrow0 + eh * 128
                    S.dma("sp", yT[r0:r0 + 128, t0 + g * 512:t0 + (g + 1) * 512], yb[eh][:], reads=[ybb[eh]],
                          writes=[yTb[(r0 // 128, t0 // 512 + g)]])

            phaseA1(0)
            pend = None
            for g in range(HTT):
                if g + 1 < HTT:
                    phaseA1(g + 1)
                sl = phaseA2(g)
                if pend is not None:
                    phaseB(*pend)
                pend = (g, sl)
            phaseB(*pend)
            if half == 0:
                sp = st["sp"]
                S.dma("sp", stS[:, hidx * 256:hidx * 256 + dv], S32[sp][:, 0:dv], reads=[S32b[sp]], writes=[stSb[hidx]])
        A.release(m)

    def outproj(w_out, gpost, gnext):
        m = A.mark()
        QT = min(1024, TOKC // 2)
        NQ = TOKC // QT
        wsb, wsbb = A.alloc([128, KC, D], BF16, nbuf=8)
        for ip in range(8):
            S.dma("pool", wsb[:, :, ip * 256:(ip + 1) * 256],
                  w_out[:, ip * 256:(ip + 1) * 256].rearrange("(k p) n -> p k n", p=128), writes=[wsbb[ip]])
        y, yb_ = A.alloc([128, KC, QT], BF16, nbuf=KC)
        ots, otbs = [], []
        for st_ in range(2):
            ot, otb = [], []
            for c in range(KC):
                t, b = A.alloc([128, 512], F32)
                ot.append(t[:])
                otb.append(b[0])
            ots.append(ot)
            otbs.append(otb)
        hin, hinb, sq, sqb, ub, ubb, sq2, sq2b = [], [], [], [], [], [], [], []
        for i in range(3):
            t, b = A.alloc([128, 512], F32)
            hin.append(t)
            hinb.append(b[0])
            t, b = A.alloc([128, 512], BF16)
            sq.append(t)
            sqb.append(b[0])
            t, b = A.alloc([128, 512], BF16)
            sq2.append(t)
            sq2b.append(b[0])
            t, b = A.alloc([128, 512], BF16)
            ub.append(t)
            ubb.append(b[0])
        rstd, rstdb = A.alloc([128, 512], F32)
        rstdb = rstdb[0]
        ssbanks = [6, 5]
        pend = None
        idx = 0
        for qi in range(NQ):
            q0 = qi * QT
            for c in range(KC):
                S.dma("sp", y[:, c, :], yT[c * 128:(c + 1) * 128, q0:q0 + QT],
                      reads=[yTb[(c, t)] for t in range(q0 // 512, (q0 + QT) // 512)], writes=[yb_[c]])
            for i in range(QT // 512):
                tt = q0 // 512 + i
                ot, otb, ssb = ots[idx % 2], otbs[idx % 2], ssbanks[idx % 2]
                if pend is not None:
                    ep_head(*pend)
                dfr = None
                for oc in range(KC):
                    bk = oc % 4
                    for k in range(KC):
                        MM(ps[bk][:], wsb[:, k, oc * 128:(oc + 1) * 128], y[:, k, i * 512:(i + 1) * 512], k == 0, k == KC - 1,
                           [wsbb[oc // 2], yb_[k]], [psb[bk]])
                    if dfr is not None:
                        MM(*dfr)
                    CP("act", ot[oc], ps[bk][:], [psb[bk]], [otb[oc]])
                    s = oc % 3
                    ACT(sq2[s][:], ot[oc], AF.Square, [otb[oc]], [sq2b[s]])
                    dfr = (ps[ssb][:], ones[:], sq2[s][:], oc == 0, oc == KC - 1, [onesb, sq2b[s]], [psb[ssb]])
                    if pend is not None:
                        ep_chunk(oc, *pend)
                MM(*dfr)
                if pend is not None:
                    ep_tail(*pend)
                pend = (tt, ot, otb, ssb, gpost, gnext, hin, hinb, sq, sqb, ub, ubb, rstd, rstdb)
                idx += 1
        epilogue(*pend)
        A.release(m)

    def mlp(l, gpost, gnext):
        m = A.mark()
        NT2 = TM // 512
        hid, hidb = A.alloc([128, 64, TM], BF16, nbuf=64)
        WP = WPool(3)
        mu = A.mark()
        for tp in range(TOKC // TM):
            tk0 = tp * TM
            A.release(mu)
            uu, uub = A.alloc([128, KC, TM], BF16, nbuf=KC)
            rl, rlb = [], []
            for i in range(3):
                t, b = A.alloc([128, 512], F32)
                rl.append(t)
                rlb.append(b[0])
            for c in range(KC):
                S.dma("sp", uu[:, c, :], uT[c * 128:(c + 1) * 128, HALO + tk0:HALO + tk0 + TM],
                      reads=[uTb[(c, 1 + tk0 // 512 + t)] for t in range(NT2)], writes=[uub[c]])
            rli = [0]
            for jb in range(32):
                wv, wb = WP.load(w1[l][:, jb * 256:jb * 256 + 256], KC, 256)
                for j2 in range(2):
                    j = jb * 2 + j2
                    for t in range(NT2):
                        bk = (0, 1, 2, 3, 6, 7)[rli[0] % 6]
                        for k in range(KC):
                            MM(ps[bk][:], wv[:, k, j2 * 128:(j2 + 1) * 128], uu[:, k, t * 512:(t + 1) * 512], k == 0, k == KC - 1,
                               [wb, uub[k]], [psb[bk]])
                        s = rli[0] % 3
                        rli[0] += 1
                        ACT(rl[s][:], ps[bk][:], AF.Relu, [psb[bk]], [rlb[s]])
                        TT("dve", hid[:, j, t * 512:(t + 1) * 512], rl[s][:], rl[s][:], ALU.mult, [rlb[s]], [hidb[j]])
            A.release(mu)
            zt, ztb, sq, sqb = [], [], [], []
            for i in range(4):
                t, b = A.alloc([128, 512], F32)
                zt.append(t)
                ztb.append(b[0])
                t, b = A.alloc([128, 512], BF16)
                sq.append(t)
                sqb.append(b[0])
            zi = 0
            dfrB = []
            for ip in range(8):
                for jq in range(4):
                    wv, wb = WP.load(w2[l][jq * 2048:(jq + 1) * 2048, ip * 256:ip * 256 + 256], KC, 256)
                    for i2 in range(2):
                        for t in range(NT2):
                            bk = i2 * NT2 + t
                            for jj in range(KC):
                                j = jq * 16 + jj
                                MM(ps[bk][:], wv[:, jj, i2 * 128:(i2 + 1) * 128], hid[:, j, t * 512:(t + 1) * 512],
                                   jq == 0 and jj == 0, jq == 3 and jj == KC - 1, [wb, hidb[j]], [psb[bk]])
                for d_ in dfrB:
                    MM(*d_)
                dfrB = []
                for i2 in range(2):
                    oc = ip * 2 + i2
                    for t in range(NT2):
                        bk = i2 * NT2 + t
                        s = zi % 4
                        zi += 1
                        CP("act", zt[s][:], ps[bk][:], [psb[bk]], [ztb[s]])
                        ACT(sq[s][:], zt[s][:], AF.Square, [ztb[s]], [sqb[s]])
                        dfrB.append((ps[4 + t][:], ones[:], sq[s][:], oc == 0, oc == KC - 1, [onesb, sqb[s]], [psb[4 + t]]))
                        S.dma("sp", zT[oc * 128:(oc + 1) * 128, t * 512:(t + 1) * 512], zt[s][:], reads=[ztb[s]],
                              writes=[zTb[(oc, t)]])
            for d_ in dfrB:
                MM(*d_)
            A.release(mu)
            ot, otb = [], []
            for c in range(KC):
                t, b = A.alloc([128, 512], F32)
                ot.append(t[:])
                otb.append(b[0])
            hin, hinb, sq, sqb, ub, ubb, osb, osbb = [], [], [], [], [], [], [], []
            for i in range(2):
                t, b = A.alloc([128, 512], F32)
                hin.append(t)
                hinb.append(b[0])
            if gnext is not None:
                for i in range(3):
                    t, b = A.alloc([128, 512], BF16)
                    sq.append(t)
                    sqb.append(b[0])
                    t, b = A.alloc([128, 512], BF16)
                    ub.append(t)
                    ubb.append(b[0])
            rstd, rstdb = A.alloc([128, 512], F32)
            rstdb = rstdb[0]
            if gnext is None:
                for i in range(2):
                    t, b = A.alloc([128, 1024], F32)
                    osb.append(t)
                    osbb.append(b[0])
            for t in range(NT2):
                tt = tk0 // 512 + t
                for c in range(KC):
                    S.dma("sp", ot[c], zT[c * 128:(c + 1) * 128, t * 512:(t + 1) * 512], reads=[zTb[(c, t)]], writes=[otb[c]])
                epilogue(tt, ot, otb, 4 + t, gpost, gnext, hin, hinb, sq, sqb, ub, ubb, rstd, rstdb, osb, osbb)
        A.release(m)

    stage0()
    mixer0(0)
    mixer0(1)
    outproj(w_out0, 16, 32)
    mlp(0, 48, 64 if NL > 1 else None)
    if NL > 1:
        if dbg:
            dbu = nc.dram_tensor("dbg_u", [D, NTOT], F32, kind="ExternalOutput").ap()
            for c in range(KC):
                ob_ = Buf()
                outbs.append(ob_)
                S.dma("pool", dbu[c * 128:(c + 1) * 128, :], uT[c * 128:(c + 1) * 128, :],
                      reads=[uTb[(c, t)] for t in range(NTT + 1)], writes=[ob_])
        if dbg:
            dbh = nc.dram_tensor("dbg_h", [D, TOKC], F32, kind="ExternalOutput").ap()
            for c in range(KC):
                ob_ = Buf()
                outbs.append(ob_)
                S.dma("sp", dbh[c * 128:(c + 1) * 128, :], hT[c * 128:(c + 1) * 128, :],
                      reads=[hTb[(c, t)] for t in range(NTT)], writes=[ob_])
        mixer1(0)
        mixer1(1)
        if dbg:
            dby = nc.dram_tensor("dbg_y", [D, TOKC], F32, kind="ExternalOutput").ap()
            for c in range(KC):
                ob_ = Buf()
                outbs.append(ob_)
                S.dma("pool", dby[c * 128:(c + 1) * 128, :], yT[c * 128:(c + 1) * 128, :],
                      reads=[yTb[(c, t)] for t in range(NTT)], writes=[ob_])
        outproj(w_out1, 80, 96)
        mlp(1, 112, None)
    S.op("sp", lambda e: e.nop(), reads=outbs)
    S.emit()
    return nc


def _consts():
    ident = np.eye(128, dtype=np.float32)
    s = np.arange(128)[:, None]
    t = np.arange(128)[None, :]
    same = (s // 64) == (t // 64)
    U2 = (same & (s <= t)).astype(np.float32)
    R = (same & (s > t)).astype(np.float32)
    return np.concatenate([ident, U2, R], axis=1)


def _bias_tables(rel_bias):
    kpos = (np.arange(5)[:, None] * 128 + np.arange(128)[None, :])
    q = np.arange(128)
    rel = (512 + q)[None, None, :] - kpos[:, :, None]
    idx = np.clip(rel, -63, 256) + 63
    qc = (512 + q) // 64
    kc = kpos // 64
    valid = (kc[:, :, None] <= qc[None, None, :]) & (kc[:, :, None] >= qc[None, None, :] - 8)
    out = np.empty((8, 128, 640), np.float32)
    for h in range(8):
        b = rel_bias[h][idx]
        b = np.where(valid, b, np.float32(NEG))
        out[h] = b.transpose(1, 0, 2).reshape(128, 640)
    return out


def _prep_shared(inp, NL):
    f = np.float32
    norm_g = np.asarray(inp["norm_g"], f)
    sh = {}
    sh["w_in0"] = np.ascontiguousarray(np.asarray(inp["even_w_in"], f)[0])
    sh["w_out0"] = np.ascontiguousarray(np.asarray(inp["even_w_out"], f)[0])
    sh["w_in1"] = np.ascontiguousarray(np.asarray(inp["odd_w_in"], f)[0])
    sh["w_out1"] = np.ascontiguousarray(np.asarray(inp["odd_w_out"], f)[0])
    for l in range(2):
        sh["w1_%d" % l] = np.ascontiguousarray(np.asarray(inp["mlp_w1"], f)[l])
        sh["w2_%d" % l] = np.ascontiguousarray(np.asarray(inp["mlp_w2"], f)[l])
    sh["gT"] = np.ascontiguousarray(norm_g.reshape(8, 16, 128).transpose(2, 0, 1).reshape(128, 128))
    sh["convT"] = np.ascontiguousarray(np.asarray(inp["even_conv_w"], f)[0].reshape(3, 8, 128).transpose(2, 0, 1).reshape(128, 24))
    sh["biasT"] = _bias_tables(np.asarray(inp["even_rel_bias"], f)[0])
    sh["cst"] = _consts()
    lb = np.asarray(inp["hgrn_lb"], f)
    sh["lbT"] = np.ascontiguousarray(lb.reshape(2, 8, 128).transpose(2, 0, 1).reshape(128, 16))
    sh["lbrow"] = np.ascontiguousarray(np.broadcast_to(lb.reshape(1, 2048), (128, 2048)))
    sh["hngT"] = np.ascontiguousarray(np.asarray(inp["hgrn_norm_g"], f)[0].reshape(8, 128).T)
    sh["gngT"] = np.ascontiguousarray(np.asarray(inp["gla_norm_g"], f)[0].reshape(8, 128).T)
    sh["wa2"] = np.ascontiguousarray(np.asarray(inp["gla_wa2"], f)[0])
    sh["barow"] = np.ascontiguousarray(np.broadcast_to(np.asarray(inp["gla_ba"], f)[0].reshape(1, 512), (128, 512)))
    return sh


def run(inp, NL=2, tokc=None, trace=False, dbg=False):
    x = np.asarray(inp["x"], np.float32)
    B, SQ, _ = x.shape
    tokc = tokc or SQ
    nc = build(tokc, NL, dbg)
    sh = _prep_shared(inp, NL)
    in_maps = []
    for b in range(B):
        m = dict(sh)
        m["x"] = np.ascontiguousarray(x[b, :tokc])
        in_maps.append(m)
    res = run_bass_kernel_spmd(nc, in_maps, core_ids=list(range(B)), trace=trace)
    outs = np.stack([np.asarray(r["out"]) for r in res.results], axis=0)
    if dbg:
        return outs.astype(np.float32), res, [(np.asarray(r["dbg_y"]), np.asarray(r["dbg_u"]), np.asarray(r["dbg_h"])) for r in res.results]
    return outs.astype(np.float32), res


def kernel(**inputs):
    o, _ = run(inputs)
    return o
```

```python
import contextlib
import numpy as np
import concourse.bass as bass
import concourse.mybir as mybir
from concourse.bass_utils import run_bass_kernel_spmd

F32 = mybir.dt.float32
BF16 = mybir.dt.bfloat16
ALU = mybir.AluOpType
AF = mybir.ActivationFunctionType

N_DMA_SEMS = 40
import os as _os0
STRICT_SAME_ENGINE = bool(int(_os0.environ.get("STRICT_SE", "1")))
D = 2048
KC = 16
DFF = 8192
HALO = 512
EPS = 1e-6
NEG = -30000.0


class Buf:
    __slots__ = ("w", "rs", "excl")

    def __init__(self, excl=False):
        self.w = None
        self.rs = []
        self.excl = excl


class Op:
    __slots__ = ("eng", "fn", "seq", "waits", "sig", "tick", "dma", "dsem", "dval", "vc", "dwaits")


class Sched:
    ENGS = ("pe", "act", "dve", "pool", "sp")

    def __init__(self, nc):
        self.nc = nc
        self.ops = {e: [] for e in self.ENGS}
        self.known = {e: {f: -1 for f in self.ENGS} for e in self.ENGS}
        self.known_dma = {e: {} for e in self.ENGS}
        self.dma_last = [None] * N_DMA_SEMS
        self.dma_cnt = [0] * N_DMA_SEMS
        self.dma_pool = {"sp": list(range(0, 24)), "pool": list(range(24, N_DMA_SEMS)), "act": []}
        self.dma_rr = {"sp": 0, "pool": 0, "act": 0}

    def _add(self, eng, fn, reads, writes, dma):
        o = Op()
        o.eng = eng
        o.fn = fn
        o.dma = dma
        o.sig = False
        o.tick = 0
        o.waits = []
        o.dwaits = []
        o.seq = len(self.ops[eng])
        kn = self.known[eng]
        kd = self.known_dma[eng]
        deps = []
        for r in reads:
            if r.w is not None:
                deps.append((r.w, True))
            if r.excl:
                for rr in r.rs:
                    if rr.eng != eng:
                        deps.append((rr, False))
        for w in writes:
            if w.w is not None:
                deps.append((w.w, False))
            for rr in w.rs:
                deps.append((rr, False))
        for d, raw in deps:
            if d.dma:
                if kd.get(d.dsem, 0) < d.dval:
                    kd[d.dsem] = d.dval
                    o.dwaits.append((d.dsem, d.dval))
            else:
                if d.eng == eng and not dma:
                    if eng == "pe" or (not raw and not STRICT_SAME_ENGINE):
                        continue
                if kn[d.eng] >= d.seq:
                    continue
                kn[d.eng] = d.seq
                for f, s in d.vc.items():
                    if f != eng and kn[f] < s:
                        kn[f] = s
                d.sig = True
                o.waits.append(d)
        if dma:
            pool_ = self.dma_pool[eng]
            k = pool_[self.dma_rr[eng] % len(pool_)]
            self.dma_rr[eng] += 1
            prev = self.dma_last[k]
            if prev is not None and kd.get(k, 0) < prev.dval:
                kd[k] = prev.dval
                o.dwaits.append((k, prev.dval))
            self.dma_cnt[k] += 16
            o.dsem = k
            o.dval = self.dma_cnt[k]
            self.dma_last[k] = o
        best = {}
        for d in o.waits:
            if d.eng not in best or best[d.eng].seq < d.seq:
                best[d.eng] = d
        o.waits = list(best.values())
        o.vc = dict(kn)
        o.vc[eng] = o.seq - 1 if dma else o.seq
        for r in reads:
            r.rs.append(o)
        for w in writes:
            w.w = o
            w.rs = []
        self.ops[eng].append(o)
        return o

    def op(self, eng, fn, reads=(), writes=()):
        return self._add(eng, fn, reads, writes, False)

    def dma(self, q, out, in_, reads=(), writes=()):
        return self._add(q, lambda e: e.dma_start(out=out, in_=in_), reads, writes, True)

    def emit(self):
        nc = self.nc
        with contextlib.ExitStack() as st:
            esem = {e: st.enter_context(nc.semaphore("s_" + e)) for e in self.ENGS}
            dsem = [st.enter_context(nc.semaphore("d%d" % i)) for i in range(N_DMA_SEMS)]
            for e in self.ENGS:
                t = 0
                for o in self.ops[e]:
                    if o.sig and not o.dma:
                        t += 1
                        o.tick = t
            block = st.enter_context(nc.Block())

            def run(ename):
                def body(eng):
                    s_own = esem[ename]
                    for o in self.ops[ename]:
                        for d in o.waits:
                            eng.wait_ge(esem[d.eng], d.tick)
                        for (k, v) in o.dwaits:
                            eng.wait_ge(dsem[k], v)
                        ins = o.fn(eng)
                        if o.dma:
                            ins.then_inc(dsem[o.dsem], 16)
                        elif o.sig:
                            ins.then_inc(s_own, 1)
                return body

            block.tensor(run("pe"))
            block.scalar(run("act"))
            block.vector(run("dve"))
            block.gpsimd(run("pool"))
            block.sync(run("sp"))


class Arena:
    def __init__(self, nc, lo, hi):
        self.nc = nc
        self.lo = lo
        self.hi = hi
        self.top = lo
        self.dead = []
        self.live = []
        self.n = 0

    def alloc(self, shape, dtype, nbuf=1):
        nb = 4 if dtype == F32 else 2
        size = nb
        for s in shape[1:]:
            size *= s
        size = (size + 63) // 64 * 64
        off = self.top
        self.top += size
        assert self.top <= self.hi, ("SBUF arena overflow", self.top, self.hi)
        self.n += 1
        t = self.nc.alloc_sbuf_tensor_at("t%d" % self.n, list(shape), dtype, offset=off)
        bufs = [Buf() for _ in range(nbuf)]
        acc = []
        keep = []
        for (lo, hi, bl) in self.dead:
            if lo < off + size and off < hi:
                for b in bl:
                    if b.w is not None:
                        acc.append(b.w)
                    acc.extend(b.rs)
                if not (off <= lo and hi <= off + size):
                    keep.append((lo, hi, bl))
            else:
                keep.append((lo, hi, bl))
        self.dead = keep
        for b in bufs:
            b.rs = list(acc)
        self.live.append((off, off + size, bufs))
        return t, bufs

    def mark(self):
        return (self.top, len(self.live))

    def release(self, m):
        top, n = m
        self.dead.extend(self.live[n:])
        del self.live[n:]
        self.top = top


def build(TOKC, NL=2, dbg=False):
    nc = bass.Bass("TRN2", target_bir_lowering=False)
    NTT = TOKC // 512
    HT = TOKC // 2
    HTT = HT // 512
    TM = min(1024, TOKC)
    NTOT = HALO + TOKC

    def din(name, shape, dt=F32):
        return nc.dram_tensor(name, list(shape), dt, kind="ExternalInput").ap()

    x = din("x", [TOKC, D])
    w_in0 = din("w_in0", [D, 6144])
    w_out0 = din("w_out0", [D, D])
    w_in1 = din("w_in1", [D, 7184])
    w_out1 = din("w_out1", [D, D])
    w1 = [din("w1_%d" % l, [D, DFF]) for l in range(2)]
    w2 = [din("w2_%d" % l, [DFF, D]) for l in range(2)]
    gT_d = din("gT", [128, 128])
    convT_d = din("convT", [128, 24])
    biasT_d = din("biasT", [8, 128, 640])
    cst_d = din("cst", [128, 3 * 128])
    lbT_d = din("lbT", [128, 16])
    lbrow_d = din("lbrow", [128, 2048])
    hngT_d = din("hngT", [128, 8])
    gngT_d = din("gngT", [128, 8])
    wa2_d = din("wa2", [16, 512])
    barow_d = din("barow", [128, 512])
    out = nc.dram_tensor("out", [TOKC, D], F32, kind="ExternalOutput").ap()

    hT = nc.dram_tensor("hT", [D, TOKC], F32).ap()
    uT = nc.dram_tensor("uT", [D, NTOT], BF16).ap()
    yT = nc.dram_tensor("yT", [D, TOKC], BF16).ap()
    zT = nc.dram_tensor("zT", [D, TM], F32).ap()
    hTb = {(c, t): Buf() for c in range(KC) for t in range(NTT)}
    uTb = {(c, t): Buf() for c in range(KC) for t in range(NTT + 1)}
    yTb = {(c, t): Buf() for c in range(KC) for t in range(NTT)}
    zTb = {(c, t): Buf() for c in range(KC) for t in range(TM // 512)}
    outbs = []
    dbgs = {}

    S = Sched(nc)
    A = Arena(nc, 16640, 229376)

    def MM(o, l, r, start, stop, reads, writes):
        S.op("pe", lambda e: e.matmul(o, l, r, start=start, stop=stop), reads, writes)

    def TR(o, i, ident, reads, writes):
        S.op("pe", lambda e: e.transpose(o, i, ident), reads, writes)

    def ACT(o, i, func, reads, writes, bias=None, scale=None):
        kw = {}
        if bias is not None:
            kw["bias"] = bias
        if scale is not None:
            kw["scale"] = scale
        S.op("act", lambda e: e.activation(o, i, func, **kw), reads, writes)

    def TT(eng, o, a, b, op, reads, writes):
        S.op(eng, lambda e: e.tensor_tensor(o, a, b, op), reads, writes)

    def TS(eng, o, a, s1, s2, op0, op1, reads, writes):
        if op1 is None:
            S.op(eng, lambda e: e.tensor_scalar(o, a, s1, None, op0), reads, writes)
        else:
            S.op(eng, lambda e: e.tensor_scalar(o, a, s1, s2, op0, op1), reads, writes)

    def STT(eng, o, a, s, b, op0, op1, reads, writes):
        assert eng == "dve"
        S.op(eng, lambda e: e.scalar_tensor_tensor(o, a, s, b, op0, op1), reads, writes)

    def CP(eng, o, i, reads, writes):
        if eng == "act":
            S.op("act", lambda e: e.copy(o, i), reads, writes)
        else:
            S.op(eng, lambda e: e.tensor_copy(o, i), reads, writes)

    def RCP(o, i, reads, writes):
        S.op("dve", lambda e: e.reciprocal(o, i), reads, writes)

    def MSET(eng, o, v, writes):
        S.op(eng, lambda e: e.memset(o, v), (), writes)

    ps = []
    psb = []
    for i in range(8):
        ps.append(nc.alloc_psum_tensor("ps%d" % i, [128, 512], F32))
        psb.append(Buf(excl=True))

    cst, cstb = A.alloc([128, 384], F32)
    cstb = cstb[0]
    S.dma("sp", cst[:], cst_d[:, :], writes=[cstb])
    ident = cst[:, 0:128]
    U2 = cst[:, 128:256]
    Rm = cst[:, 256:384]
    ones, onesb = A.alloc([128, 128], BF16)
    onesb = onesb[0]
    MSET("dve", ones[:], 1.0, [onesb])
    gT, gTb = A.alloc([128, 128], F32)
    gTb = gTb[0]
    S.dma("sp", gT[:], gT_d[:, :], writes=[gTb])
    convT, convTb = A.alloc([128, 24], F32)
    convTb = convTb[0]
    S.dma("sp", convT[:], convT_d[:, :], writes=[convTb])
    zbf, zbfb = A.alloc([128, 512], BF16)
    zbfb = zbfb[0]
    MSET("dve", zbf[:], 0.0, [zbfb])
    negc, negcb = A.alloc([128, 1], F32)
    negcb = negcb[0]
    MSET("dve", negc[:], NEG, [negcb])

    for c in range(KC):
        S.dma("sp", uT[c * 128:(c + 1) * 128, 0:HALO], zbf[:, 0:HALO], reads=[zbfb], writes=[uTb[(c, 0)]])

    class WPool:
        def __init__(self, n, elems=4096):
            self.t = []
            self.prev = []
            for i in range(n):
                t, b = A.alloc([128, elems], BF16)
                self.t.append(t)
                self.prev.append(b)
            self.i = 0

        def load_multi(self, blocks, K):
            i = self.i
            self.i = (i + 1) % len(self.t)
            tot = sum(n for _, n in blocks)
            v = self.t[i][:, 0:K * tot].rearrange("p (k n) -> p k n", k=K)
            haz = []
            for b in self.prev[i]:
                if b.w is not None:
                    haz.append(b.w)
                haz.extend(b.rs)
            bufs = []
            c0 = 0
            for src, n in blocks:
                b = Buf()
                b.rs = list(haz)
                S.dma("pool", v[:, :, c0:c0 + n], src.rearrange("(k p) n -> p k n", p=128), writes=[b])
                bufs.append(b)
                c0 += n
            self.prev[i][:] = bufs
            return v, bufs

        def load(self, src_ap, K, ncols):
            v, bufs = self.load_multi([(src_ap, ncols)], K)
            return v, bufs[0]

    eng_rr = [0]

    def alt(*engs):
        eng_rr[0] += 1
        return engs[eng_rr[0] % len(engs)]

    UQ = ["pool"]

    def norm_sq(ht, htb, c, sq, sqb):
        s = c % len(sq)
        ACT(sq[s][:], ht[c], AF.Square, [htb[c]], [sqb[s]])
        MM(ps[7][:], ones[:], sq[s][:], c == 0, c == KC - 1, [onesb, sqb[s]], [psb[7]])

    def norm_to_u(ht, htb, gcol, tt_u, sq, sqb, ub, ubb, rstd, rstdb, presq=False):
        ssb = 7
        if not presq:
            for c in range(KC):
                norm_sq(ht, htb, c, sq, sqb)
        ACT(rstd[:], ps[ssb][:], AF.Ln, [psb[ssb]], [rstdb], bias=EPS, scale=1.0 / D)
        ACT(rstd[:], rstd[:], AF.Exp, [rstdb], [rstdb], scale=-0.5)
        for c in range(KC):
            s = c % len(ub)
            STT("dve", ub[s][:], ht[c], gT[:, gcol + c:gcol + c + 1], rstd[:], ALU.mult, ALU.mult,
                [htb[c], gTb, rstdb], [ubb[s]])
            S.dma(UQ[0], uT[c * 128:(c + 1) * 128, tt_u * 512:(tt_u + 1) * 512], ub[s][:],
                  reads=[ubb[s]], writes=[uTb[(c, tt_u)]])

    def ep_head(tt, ot, otb, ssbank, gpost, gnext, hin, hinb, sq, sqb, ub, ubb, rstd, rstdb, osb=None, osbb=None):
        ACT(rstd[:], ps[ssbank][:], AF.Ln, [psb[ssbank]], [rstdb], bias=EPS, scale=1.0 / D)
        ACT(rstd[:], rstd[:], AF.Exp, [rstdb], [rstdb], scale=-0.5)

    def ep_chunk(c, tt, ot, otb, ssbank, gpost, gnext, hin, hinb, sq, sqb, ub, ubb, rstd, rstdb, osb=None, osbb=None):
        s = c % len(hin)
        S.dma("sp", hin[s][:], hT[c * 128:(c + 1) * 128, tt * 512:(tt + 1) * 512],
              reads=[hTb[(c, tt)]], writes=[hinb[s]])
        STT("dve", ot[c], ot[c], gT[:, gpost + c:gpost + c + 1], rstd[:], ALU.mult, ALU.mult,
            [otb[c], gTb, rstdb], [otb[c]])
        TT("dve", ot[c], ot[c], hin[s][:], ALU.add, [otb[c], hinb[s]], [otb[c]])
        if gnext is not None:
            norm_sq(ot, otb, c, sq, sqb)

    def epilogue(tt, ot, otb, ssbank, gpost, gnext, hin, hinb, sq, sqb, ub, ubb, rstd, rstdb, osb=None, osbb=None):
        args = (tt, ot, otb, ssbank, gpost, gnext, hin, hinb, sq, sqb, ub, ubb, rstd, rstdb, osb, osbb)
        ep_head(*args)
        for c in range(KC):
            ep_chunk(c, *args)
        ep_tail(*args)

    def ep_tail(tt, ot, otb, ssbank, gpost, gnext, hin, hinb, sq, sqb, ub, ubb, rstd, rstdb, osb=None, osbb=None):
        if gnext is not None:
            for c in range(KC):
                S.dma("sp", hT[c * 128:(c + 1) * 128, tt * 512:(tt + 1) * 512], ot[c],
                      reads=[otb[c]], writes=[hTb[(c, tt)]])
            norm_to_u(ot, otb, gnext, tt + 1, sq, sqb, ub, ubb, rstd, rstdb, presq=True)
        else:
            for tb in range(4):
                r0 = tt * 512 + tb * 128
                for ch in range(2):
                    s = ch
                    for c2 in range(2):
                        c4 = ch * 2 + c2
                        bank = c4 % 2
                        for cc in range(4):
                            c = c4 * 4 + cc
                            TR(ps[bank][:, cc * 128:(cc + 1) * 128], ot[c][:, tb * 128:(tb + 1) * 128], ident,
                               [otb[c], cstb], [psb[bank]])
                        CP(alt("act", "dve"), osb[s][:, c2 * 512:(c2 + 1) * 512], ps[bank][:], [psb[bank]], [osbb[s]])
                    ob_ = Buf()
                    outbs.append(ob_)
                    S.dma("sp", out[r0:r0 + 128, ch * 1024:(ch + 1) * 1024], osb[s][:], reads=[osbb[s]], writes=[ob_])

    def stage0():
        m = A.mark()
        xs = []
        xsb = []
        for i in range(2):
            t, b = A.alloc([128, 4, D], F32, nbuf=4)
            xs.append(t)
            xsb.append(b)
        ht = []
        htb = []
        for c in range(KC):
            t, b = A.alloc([128, 512], F32)
            ht.append(t[:])
            htb.append(b[0])
        sq, sqb, ub, ubb = [], [], [], []
        for i in range(3):
            t, b = A.alloc([128, 512], BF16)
            sq.append(t)
            sqb.append(b[0])
            t, b = A.alloc([128, 512], BF16)
            ub.append(t)
            ubb.append(b[0])
        rstd, rstdb = A.alloc([128, 512], F32)
        rstdb = rstdb[0]
        for tt in range(NTT):
            s = tt % 2
            for tb in range(4):
                r0 = (tt * 4 + tb) * 128
                S.dma("sp", xs[s][:, tb, :], x[r0:r0 + 128, :], writes=[xsb[s][tb]])
            for c in range(KC):
                bank = c % 4
                for tb in range(4):
                    TR(ps[bank][:, tb * 128:(tb + 1) * 128], xs[s][:, tb, c * 128:(c + 1) * 128], ident,
                       [xsb[s][tb], cstb], [psb[bank]])
                CP(alt("act", "dve"), ht[c], ps[bank][:], [psb[bank]], [htb[c]])
                S.dma("pool", hT[c * 128:(c + 1) * 128, tt * 512:(tt + 1) * 512], ht[c],
                      reads=[htb[c]], writes=[hTb[(c, tt)]])
            norm_to_u(ht, htb, 0, tt + 1, sq, sqb, ub, ubb, rstd, rstdb)
        A.release(m)

    def mixer0(half):
        m = A.mark()
        t0 = half * HT
        NH = HALO + HT
        u, ub_ = A.alloc([128, KC, NH], BF16, nbuf=KC)
        for c in range(KC):
            rd = [uTb[(c, t)] for t in range(t0 // 512, t0 // 512 + NH // 512)]
            S.dma("sp", u[:, c, :], uT[c * 128:(c + 1) * 128, t0:t0 + NH], reads=rd, writes=[ub_[c]])
        WP = WPool(4)
        ybf = []
        ybfb = []
        for i in range(2):
            t, b = A.alloc([128, HT], BF16)
            ybf.append(t)
            ybfb.append(b[0])
        yi = [0]

        def proj_fm(wv, wb, col, tok0, ntok, bank):
            for k in range(KC):
                MM(ps[bank][:, 0:ntok], wv[:, k, col:col + 128], u[:, k, tok0:tok0 + ntok], k == 0, k == KC - 1,
                   [wb, ub_[k]], [psb[bank]])

        mA = A.mark()
        NCV = HT + 128
        ct, ctb = A.alloc([128, NCV], F32)
        ctb = ctb[0]
        ucv, ucvb = A.alloc([128, NCV], F32)
        ucvb = ucvb[0]
        acc, accb = A.alloc([128, HT], F32)
        accb = accb[0]
        for gp in range(4):
            wbv, wbb = WP.load(w_in0[:, gp * 256:gp * 256 + 256], KC, 256)
            wcv, wcb = WP.load(w_in0[:, 1024 + gp * 256:1024 + gp * 256 + 256], KC, 256)
            whv, whb = WP.load(w_in0[:, 2048 + gp * 256:2048 + gp * 256 + 256], KC, 256)
            for g2 in range(2):
                g = gp * 2 + g2
                col = g2 * 128
                pieces = [(HALO - 128, 128)] + [(HALO + i * 512, 512) for i in range(HTT)]
                for pi, (tk, n) in enumerate(pieces):
                    o0 = tk - (HALO - 128)
                    bk = pi % 2
                    proj_fm(wcv, wcb, col, tk, n, bk)
                    CP("act", ct[:, o0:o0 + n], ps[bk][:, 0:n], [psb[bk]], [ctb])
                    bk2 = 2 + pi % 2
                    proj_fm(whv, whb, col, tk, n, bk2)
                    TT("dve", ucv[:, o0:o0 + n], ps[bk2][:, 0:n], ct[:, o0:o0 + n], ALU.mult, [psb[bk2], ctb], [ucvb])
                TS("dve", acc[:], ucv[:, 126:126 + HT], convT[:, g:g + 1], None, ALU.mult, None, [ucvb, convTb], [accb])
                STT("dve", acc[:], ucv[:, 127:127 + HT], convT[:, 8 + g:9 + g], acc[:], ALU.mult, ALU.add,
                    [ucvb, convTb, accb], [accb])
                STT("dve", acc[:], ucv[:, 128:128 + HT], convT[:, 16 + g:17 + g], acc[:], ALU.mult, ALU.add,
                    [ucvb, convTb, accb], [accb])
                ys = yi[0] % 2
                yi[0] += 1
                for i in range(HTT):
                    bk = 4 + i % 2
                    proj_fm(wbv, wbb, col, HALO + i * 512, 512, bk)
                    TT("dve", ybf[ys][:, i * 512:(i + 1) * 512], ps[bk][:], acc[:, i * 512:(i + 1) * 512], ALU.mult,
                       [psb[bk], accb], [ybfb[ys]])
                S.dma("sp", yT[g * 128:(g + 1) * 128, t0:t0 + HT], ybf[ys][:], reads=[ybfb[ys]],
                      writes=[yTb[(g, t)] for t in range(t0 // 512, t0 // 512 + HTT)])
        A.release(mA)

        NKB = NH // 128
        qT, qTb = A.alloc([128, HT], BF16)
        qTb = qTb[0]
        kT, kTb = A.alloc([128, NH], BF16)
        kTb = kTb[0]
        vt, vtb = A.alloc([128, NKB, 256], BF16)
        vtb = vtb[0]
        bT = []
        bTb = []
        for i in range(2):
            t, b = A.alloc([128, 640], F32)
            bT.append(t)
            bTb.append(b[0])
        et, etb, pT, pTb, rd_, rdb = [], [], [], [], [], []
        for i in range(2):
            t, b = A.alloc([128, 640], F32)
            et.append(t)
            etb.append(b[0])
            t, b = A.alloc([128, 640], BF16)
            pT.append(t)
            pTb.append(b[0])
            t, b = A.alloc([128, 128], F32)
            rd_.append(t)
            rdb.append(b[0])
        scale = 128.0 ** -0.5
        for hp in range(4):
            wqv, wqb = WP.load(w_in0[:, 3072 + hp * 256:3072 + hp * 256 + 256], KC, 256)
            wkv, wkb = WP.load(w_in0[:, 4096 + hp * 256:4096 + hp * 256 + 256], KC, 256)
            wvv, wvb = WP.load(w_in0[:, 5120 + hp * 256:5120 + hp * 256 + 256], KC, 256)
            for tb in range(NKB):
                bk = tb % 2
                for k in range(KC):
                    MM(ps[bk][:, 0:256], u[:, k, tb * 128:(tb + 1) * 128], wvv[:, k, :], k == 0, k == KC - 1,
                       [ub_[k], wvb], [psb[bk]])
                CP(alt("act", "dve"), vt[:, tb, :], ps[bk][:, 0:256], [psb[bk]], [vtb])
            for h2 in range(2):
                hd = hp * 2 + h2
                col = h2 * 128
                bs = hd % 2
                S.dma("sp", bT[bs][:], biasT_d[hd], writes=[bTb[bs]])
                for i in range(HTT):
                    bk = 2 + i % 2
                    proj_fm(wqv, wqb, col, HALO + i * 512, 512, bk)
                    ACT(qT[:, i * 512:(i + 1) * 512], ps[bk][:], AF.Copy, [psb[bk]], [qTb], scale=scale)
                for i in range(NH // 512):
                    bk = 2 + i % 2
                    proj_fm(wkv, wkb, col, i * 512, 512, bk)
                    CP(alt("act", "dve"), kT[:, i * 512:(i + 1) * 512], ps[bk][:], [psb[bk]], [kTb])
                ys = yi[0] % 2
                yi[0] += 1
                for qt in range(HT // 128):
                    s = qt % 2
                    bx = 4 - 2 * (qt % 2)
                    bo = 6 + qt % 2
                    for kb in range(5):
                        dst = ps[bx][:, kb * 128:(kb + 1) * 128] if kb < 4 else ps[bx + 1][:, 0:128]
                        dbuf = psb[bx] if kb < 4 else psb[bx + 1]
                        MM(dst, kT[:, (qt + kb) * 128:(qt + kb + 1) * 128], qT[:, qt * 128:(qt + 1) * 128], True, True,
                           [kTb, qTb], [dbuf])
                    TT("dve", et[s][:, 0:512], ps[bx][:], bT[bs][:, 0:512], ALU.add, [psb[bx], bTb[bs]], [etb[s]])
                    TT("dve", et[s][:, 512:640], ps[bx + 1][:, 0:128], bT[bs][:, 512:640], ALU.add,
                       [psb[bx + 1], bTb[bs]], [etb[s]])
                    nh = max(0, 4 - qt) if half == 0 else 0
                    if nh > 0:
                        ACT(pT[s][:, 0:nh * 128], et[s][:, 0:nh * 128], AF.Exp, [etb[s], negcb], [pTb[s]], bias=negc[:])
                    ACT(pT[s][:, nh * 128:640], et[s][:, nh * 128:640], AF.Exp, [etb[s]], [pTb[s]])
                    for kb in range(5):
                        MM(ps[bo][:, 0:128], vt[:, qt + kb, col:col + 128], pT[s][:, kb * 128:(kb + 1) * 128], kb == 0, kb == 4,
                           [vtb, pTb[s]], [psb[bo]])
                    for kb in range(5):
                        MM(ps[bo][:, 128:256], ones[:], pT[s][:, kb * 128:(kb + 1) * 128], kb == 0, kb == 4,
                           [onesb, pTb[s]], [psb[bo]])
                    RCP(rd_[s][:], ps[bo][:, 128:256], [psb[bo]], [rdb[s]])
                    TT("dve", ybf[ys][:, qt * 128:(qt + 1) * 128], ps[bo][:, 0:128], rd_[s][:], ALU.mult,
                       [psb[bo], rdb[s]], [ybfb[ys]])
                c = 8 + hd
                S.dma("sp", yT[c * 128:(c + 1) * 128, t0:t0 + HT], ybf[ys][:], reads=[ybfb[ys]],
                      writes=[yTb[(c, t)] for t in range(t0 // 512, t0 // 512 + HTT)])
        A.release(m)


    stS = nc.dram_tensor("stS", [128, 12 * 256], F32).ap()
    stSb = [Buf() for _ in range(12)]

    def ring(shape, dtype, n=2):
        ts, bs = [], []
        for i in range(n):
            t, b = A.alloc(shape, dtype)
            ts.append(t)
            bs.append(b[0])
        return ts, bs

    def mixer1(half):
        m = A.mark()
        t0 = half * HT
        u, ub_ = A.alloc([128, KC, HT], BF16, nbuf=KC)
        for c in range(KC):
            S.dma("sp", u[:, c, :], uT[c * 128:(c + 1) * 128, HALO + t0:HALO + t0 + HT],
                  reads=[uTb[(c, 1 + t0 // 512 + t)] for t in range(HTT)], writes=[ub_[c]])
        WP = WPool(3, 8192)
        lbr, lbrb = A.alloc([128, 1024], F32)
        lbrb = lbrb[0]
        omlr, omlrb = A.alloc([128, 1024], F32)
        omlrb = omlrb[0]
        lbc, lbcb = A.alloc([128, 8], F32)
        lbcb = lbcb[0]
        omlc, omlcb = A.alloc([128, 8], F32)
        omlcb = omlcb[0]
        nomlc, nomlcb = A.alloc([128, 8], F32)
        nomlcb = nomlcb[0]
        mt = A.mark()
        lr, lrb = A.alloc([128, 2048], F32)
        lrb = lrb[0]
        lc, lcb = A.alloc([128, 16], F32)
        lcb = lcb[0]
        S.dma("sp", lr[:], lbrow_d[:, :], writes=[lrb])
        S.dma("sp", lc[:], lbT_d[:, :], writes=[lcb])
        TT("dve", lbr[:], lr[:, 1024:2048], lr[:, 0:1024], ALU.subtract, [lrb], [lbrb])
        ACT(lbr[:], lbr[:], AF.Exp, [lbrb], [lbrb], scale=-1.0)
        TS("pool", lbr[:], lbr[:], 1.0, None, ALU.add, None, [lbrb], [lbrb])
        RCP(lbr[:], lbr[:], [lbrb], [lbrb])
        TS("pool", omlr[:], lbr[:], -1.0, 1.0, ALU.mult, ALU.add, [lbrb], [omlrb])
        TT("dve", lbc[:], lc[:, 8:16], lc[:, 0:8], ALU.subtract, [lcb], [lbcb])
        ACT(lbc[:], lbc[:], AF.Exp, [lbcb], [lbcb], scale=-1.0)
        TS("pool", lbc[:], lbc[:], 1.0, None, ALU.add, None, [lbcb], [lbcb])
        RCP(lbc[:], lbc[:], [lbcb], [lbcb])
        TS("pool", omlc[:], lbc[:], -1.0, 1.0, ALU.mult, ALU.add, [lbcb], [omlcb])
        TS("pool", nomlc[:], lbc[:], -1.0, None, ALU.add, None, [lbcb], [nomlcb])
        A.release(mt)
        wa2s, wa2sb = A.alloc([16, 512], F32)
        wa2sb = wa2sb[0]
        S.dma("sp", wa2s[:], wa2_d[:, :], writes=[wa2sb])
        barow, barowb = A.alloc([128, 512], F32)
        barowb = barowb[0]
        S.dma("sp", barow[:], barow_d[:, :], writes=[barowb])
        hng, hngb = A.alloc([128, 8], F32)
        hngb = hngb[0]
        S.dma("sp", hng[:], hngT_d[:, :], writes=[hngb])
        gng, gngb = A.alloc([128, 8], F32)
        gngb = gngb[0]
        S.dma("sp", gng[:], gngT_d[:, :], writes=[gngb])
        gaT, gaTb = A.alloc([16, HT], F32)
        gaTb = gaTb[0]
        wga, wgab = WP.load(w_in1[:, 7168:7184], KC, 16)
        for g in range(HTT):
            for k in range(KC):
                MM(ps[7][0:16, :], wga[:, k, 0:16], u[:, k, g * 512:(g + 1) * 512], k == 0, k == KC - 1,
                   [wgab, ub_[k]], [psb[7]])
            CP("act", gaT[0:16, g * 512:(g + 1) * 512], ps[7][0:16, :], [psb[7]], [gaTb])
        qs, qsb = ring([128, 512], F32, 2)
        kf, kfb = ring([128, 512], F32, 2)
        sg4, sg4b = ring([128, 512], F32, 2)
        tA, tAb = ring([128, 512], F32, 2)
        gs0, gs0b = ring([128, 512], F32, 1)
        gs1, gs1b = ring([128, 512], F32, 1)
        oh0, oh0b = ring([128, 512], F32, 1)
        oh1, oh1b = ring([128, 512], F32, 1)
        sqr, sqrb = ring([128, 512], BF16)
        rs, rsb = ring([128, 512], F32, 1)
        yb0, yb0b = ring([128, 512], BF16)
        yb1, yb1b = ring([128, 512], BF16)
        X1, X1b = ring([128, 512], F32, 1)
        X2, X2b = ring([128, 512], F32, 1)
        la4, la4b = ring([128, 512], F32, 1)
        kt4, kt4b = ring([128, 512], F32, 2)
        kh4, kh4b = ring([128, 512], BF16, 1)
        ktl4, ktl4b = ring([128, 512], BF16, 1)
        vt4, vt4b = ring([128, 4, 256], BF16, 3)
        qtl4, qtl4b = ring([128, 512], BF16)
        AT4, AT4b = ring([128, 512], BF16)
        c4a, c4ab = ring([128, 512], F32, 1)
        c4b, c4bb = ring([128, 512], F32, 1)
        U4, U4b = ring([128, 512], F32, 1)
        for j in range(4):
            CP("act", U4[0][:, j * 128:(j + 1) * 128], U2, [cstb], [U4b[0]])
        S32, S32b = ring([128, 256], F32)
        Sring, Sringb = ring([128, 256], BF16, 18)
        qscale = 128.0 ** -0.5

        def silu_from_psum(bank, dst, dstb, ti=0):
            ACT(tA[ti][:], ps[bank][:], AF.Sigmoid, [psb[bank]], [tAb[ti]])
            TT("dve", dst, ps[bank][:], tA[ti][:], ALU.mult, [psb[bank], tAb[ti]], [dstb])

        heads = [("h", i) for i in range(8)] + [("g", i) for i in range(4)]
        import os as _os
        if _os.environ.get("DBG_HEADS"):
            heads = [h for h in heads if h[0] in _os.environ["DBG_HEADS"]][:int(_os.environ.get("DBG_NH", "12"))]
        scount = [0]
        for hidx, (kind, hi) in enumerate(heads):
            dv = 128 if kind == "h" else 256
            NE = dv // 128
            if kind == "h":
                tokb = [(w_in1[:, 1024 + hi * 128:1024 + hi * 128 + 128], 128),
                        (w_in1[:, 2048 + hi * 128:2048 + hi * 128 + 128], 128)]
                fmb = [(w_in1[:, hi * 128:hi * 128 + 128], 128),
                       (w_in1[:, 1024 + hi * 128:1024 + hi * 128 + 128], 128),
                       (w_in1[:, 3072 + hi * 128:3072 + hi * 128 + 128], 128)]
                yrow0 = hi * 128
                ng, ngb, ngc = hng, hngb, hi
                for j in range(4):
                    CP("act", c4a[0][:, j * 128:(j + 1) * 128], omlr[:, hi * 128:(hi + 1) * 128], [omlrb], [c4ab[0]])
                    CP("act", c4b[0][:, j * 128:(j + 1) * 128], lbr[:, hi * 128:(hi + 1) * 128], [lbrb], [c4bb[0]])
            else:
                tokb = [(w_in1[:, 4608 + hi * 128:4608 + hi * 128 + 128], 128),
                        (w_in1[:, 5120 + hi * 256:5120 + hi * 256 + 256], 256)]
                fmb = [(w_in1[:, 4096 + hi * 128:4096 + hi * 128 + 128], 128),
                       (w_in1[:, 4608 + hi * 128:4608 + hi * 128 + 128], 128),
                       (w_in1[:, 6144 + hi * 256:6144 + hi * 256 + 256], 256)]
                yrow0 = 1024 + hi * 256
                ng, ngb, ngc = gng, gngb, hi * 2
                for j in range(4):
                    CP("act", c4a[0][:, j * 128:(j + 1) * 128], barow[:, hi * 128:(hi + 1) * 128], [barowb], [c4ab[0]])
            wt, wtb = WP.load_multi(tokb, KC)
            wf, wfb = WP.load_multi(fmb, KC)
            st = {"sp": 0}
            sl0 = scount[0] % 18
            scount[0] += 1
            if half == 0:
                MSET("dve", S32[0][:, 0:dv], 0.0, [S32b[0]])
                MSET("pool", Sring[sl0][:, 0:dv], 0.0, [Sringb[sl0]])
            else:
                S.dma("sp", S32[0][:, 0:dv], stS[:, hidx * 256:hidx * 256 + dv], reads=[stSb[hidx]], writes=[S32b[0]])
                CP("act", Sring[sl0][:, 0:dv], S32[0][:, 0:dv], [S32b[0]], [Sringb[sl0]])
            cur = {"slot": sl0}

            def phaseA1(g):
                r = g % 2
                r3 = g % 3
                gsl = slice(g * 512, (g + 1) * 512)
                for k in range(KC):
                    MM(ps[0][:], wf[:, k, 0:128], u[:, k, gsl], k == 0, k == KC - 1, [wfb[0], ub_[k]], [psb[0]])
                if kind == "h":
                    silu_from_psum(0, qs[r][:], qsb[r])
                else:
                    ACT(qs[r][:], ps[0][:], AF.Copy, [psb[0]], [qsb[r]], scale=qscale)
                for k in range(KC):
                    MM(ps[1][:], wf[:, k, 128:256], u[:, k, gsl], k == 0, k == KC - 1, [wfb[1], ub_[k]], [psb[1]])
                if kind == "h":
                    ACT(tA[0][:], ps[1][:], AF.Sigmoid, [psb[1]], [tAb[0]])
                    TS("dve", kf[r][:], tA[0][:], nomlc[:, hi:hi + 1], omlc[:, hi:hi + 1], ALU.mult, ALU.add,
                       [tAb[0], nomlcb, omlcb], [kfb[r]])
                else:
                    CP("act", kf[r][:], ps[1][:], [psb[1]], [kfb[r]])
                for tl in range(4):
                    tok = (g * 4 + tl) * 128
                    bk = tl % 2
                    for k in range(KC):
                        MM(ps[bk][:, 0:128 + dv], u[:, k, tok:tok + 128], wt[:, k, :], k == 0, k == KC - 1,
                           [ub_[k], wtb[0], wtb[1]], [psb[bk]])
                    CP("act", vt4[r3][:, tl, 0:dv], ps[bk][:, 128:128 + dv], [psb[bk]], [vt4b[r3]])
                    if kind == "h":
                        ACT(sg4[r][:, tl * 128:(tl + 1) * 128], ps[bk][:, 0:128], AF.Sigmoid, [psb[bk]], [sg4b[r]])
                    else:
                        CP("dve", kt4[r][:, tl * 128:(tl + 1) * 128], ps[bk][:, 0:128], [psb[bk]], [kt4b[r]])

            def phaseA2(g):
                r = g % 2
                r3 = g % 3
                if kind == "h":
                    TT("pool", X2[0][:], sg4[r][:], c4a[0][:], ALU.mult, [sg4b[r], c4ab[0]], [X2b[0]])
                    TT("pool", X2[0][:], X2[0][:], c4b[0][:], ALU.add, [X2b[0], c4bb[0]], [X2b[0]])
                    ACT(la4[0][:], X2[0][:], AF.Ln, [X2b[0]], [la4b[0]])
                    TS("pool", kt4[r][:], X2[0][:], -1.0, 1.0, ALU.mult, ALU.add, [X2b[0]], [kt4b[r]])
                else:
                    for tl in range(4):
                        tok = (g * 4 + tl) * 128
                        MM(ps[2][:, tl * 128:(tl + 1) * 128], gaT[0:16, tok:tok + 128], wa2s[0:16, hi * 128:(hi + 1) * 128],
                           True, True, [gaTb, wa2sb], [psb[2]])
                    TT("dve", X1[0][:], ps[2][:], c4a[0][:], ALU.add, [psb[2], c4ab[0]], [X1b[0]])
                    ACT(X1[0][:], X1[0][:], AF.Exp, [X1b[0]], [X1b[0]], scale=-1.0)
                    ACT(X2[0][:], X1[0][:], AF.Ln, [X1b[0]], [X2b[0]], bias=1.0)
                    TS("pool", la4[0][:], X2[0][:], -1.0 / 16.0, None, ALU.mult, None, [X2b[0]], [la4b[0]])
                MM(ps[2][:], Rm, la4[0][:], True, True, [cstb, la4b[0]], [psb[2]])
                ACT(X1[0][:], ps[2][:], AF.Exp, [psb[2]], [X1b[0]])
                TT("pool", kh4[0][:], kt4[r][:], X1[0][:], ALU.mult, [kt4b[r], X1b[0]], [kh4b[0]])
                for tl in range(4):
                    MM(ps[3][:, tl * 128:(tl + 1) * 128], la4[0][:, tl * 128:(tl + 1) * 128], U2, True, True,
                       [la4b[0], cstb], [psb[3]])
                ACT(X2[0][:], ps[3][:], AF.Exp, [psb[3]], [X2b[0]])
                ACT(X1[0][:], ps[3][:], AF.Exp, [psb[3]], [X1b[0]], scale=-1.0)
                TT("dve", qtl4[r][:], qs[r][:], X2[0][:], ALU.mult, [qsb[r], X2b[0]], [qtl4b[r]])
                TT("pool", ktl4[0][:], kf[r][:], X1[0][:], ALU.mult, [kfb[r], X1b[0]], [ktl4b[0]])
                for tl in range(4):
                    MM(ps[4][:, tl * 128:(tl + 1) * 128], ktl4[0][:, tl * 128:(tl + 1) * 128], qtl4[r][:, tl * 128:(tl + 1) * 128],
                       True, True, [ktl4b[0], qtl4b[r]], [psb[4]])
                TT("dve", AT4[r][:], ps[4][:], U4[0][:], ALU.mult, [psb[4], U4b[0]], [AT4b[r]])
                slots = []
                for tl in range(4):
                    pc = (tl % 2) * dv
                    for ch in range(2):
                        MM(ps[6 + ch][:, pc:pc + dv], kh4[0][ch * 64:ch * 64 + 64, tl * 128:(tl + 1) * 128],
                           vt4[r3][ch * 64:ch * 64 + 64, tl, 0:dv], True, True, [kh4b[0], vt4b[r3]], [psb[6 + ch]])
                    for ch in range(2):
                        slots.append(cur["slot"])
                        sp = st["sp"]
                        col = tl * 128 + ch * 64 + 63
                        STT("dve", S32[1 - sp][:, 0:dv], S32[sp][:, 0:dv], X2[0][:, col:col + 1], ps[6 + ch][:, pc:pc + dv],
                            ALU.mult, ALU.add, [S32b[sp], X2b[0], psb[6 + ch]], [S32b[1 - sp]])
                        st["sp"] = 1 - sp
                        ns = scount[0] % 18
                        scount[0] += 1
                        CP("act", Sring[ns][:, 0:dv], S32[1 - sp][:, 0:dv], [S32b[1 - sp]], [Sringb[ns]])
                        cur["slot"] = ns
                return slots

            def phaseB(g, slots):
                r = g % 2
                r3 = g % 3
                gsl = slice(g * 512, (g + 1) * 512)
                gsv = [gs0[0], gs1[0]]
                gsvb = [gs0b[0], gs1b[0]]
                for eh in range(NE):
                    bk = eh % 2
                    for k in range(KC):
                        MM(ps[bk][:], wf[:, k, 256 + eh * 128:256 + (eh + 1) * 128], u[:, k, gsl], k == 0, k == KC - 1,
                           [wfb[2], ub_[k]], [psb[bk]])
                    silu_from_psum(bk, gsv[eh][:], gsvb[eh], 1)
                ohv = [oh0[0], oh1[0]]
                ohvb = [oh0b[0], oh1b[0]]
                obank = [5, 3]
                for eh in range(NE):
                    ob = obank[eh]
                    for tl in range(4):
                        MM(ps[ob][:, tl * 128:(tl + 1) * 128], vt4[r3][:, tl, eh * 128:(eh + 1) * 128], AT4[r][:, tl * 128:(tl + 1) * 128],
                           tl == 0, False, [vt4b[r3], AT4b[r]], [psb[ob]])
                    for tl in range(4):
                        for ch in range(2):
                            sl = slots[tl * 2 + ch]
                            c0 = tl * 128 + ch * 64
                            MM(ps[ob][:, c0:c0 + 64], Sring[sl][:, eh * 128:(eh + 1) * 128], qtl4[r][:, c0:c0 + 64],
                               False, tl == 3 and ch == 1, [Sringb[sl], qtl4b[r]], [psb[ob]])
                    CP("act", ohv[eh][:], ps[ob][:], [psb[ob]], [ohvb[eh]])
                for eh in range(NE):
                    ACT(sqr[eh][:], ohv[eh][:], AF.Square, [ohvb[eh]], [sqrb[eh]])
                    MM(ps[2][:], ones[:], sqr[eh][:], eh == 0, eh == NE - 1, [onesb, sqrb[eh]], [psb[2]])
                ACT(rs[0][:], ps[2][:], AF.Ln, [psb[2]], [rsb[0]], bias=EPS, scale=1.0 / dv)
                ACT(rs[0][:], rs[0][:], AF.Exp, [rsb[0]], [rsb[0]], scale=-0.5)
                yb = [yb0[r], yb1[r]]
                ybb = [yb0b[r], yb1b[r]]
                for eh in range(NE):
                    STT("dve", ohv[eh][:], ohv[eh][:], ng[:, ngc + eh:ngc + eh + 1], rs[0][:], ALU.mult, ALU.mult,
                        [ohvb[eh], ngb, rsb[0]], [ohvb[eh]])
                    TT("pool", yb[eh][:], ohv[eh][:], gsv[eh][:], ALU.mult, [ohvb[eh], gsvb[eh]], [ybb[eh]])
                    r0 = yrow0 + eh * 128
                    S.dma("sp", yT[r0:r0 + 128, t0 + g * 512:t0 + (g + 1) * 512], yb[eh][:], reads=[ybb[eh]],
                          writes=[yTb[(r0 // 128, t0 // 512 + g)]])

            phaseA1(0)
            pend = None
            for g in range(HTT):
                if g + 1 < HTT:
                    phaseA1(g + 1)
                sl = phaseA2(g)
                if pend is not None:
                    phaseB(*pend)
                pend = (g, sl)
            phaseB(*pend)
            if half == 0:
                sp = st["sp"]
                S.dma("sp", stS[:, hidx * 256:hidx * 256 + dv], S32[sp][:, 0:dv], reads=[S32b[sp]], writes=[stSb[hidx]])
        A.release(m)

    def outproj(w_out, gpost, gnext):
        m = A.mark()
        QT = min(1024, TOKC // 2)
        NQ = TOKC // QT
        wsb, wsbb = A.alloc([128, KC, D], BF16, nbuf=8)
        for ip in range(8):
            S.dma("pool", wsb[:, :, ip * 256:(ip + 1) * 256],
                  w_out[:, ip * 256:(ip + 1) * 256].rearrange("(k p) n -> p k n", p=128), writes=[wsbb[ip]])
        y, yb_ = A.alloc([128, KC, QT], BF16, nbuf=KC)
        ots, otbs = [], []
        for st_ in range(2):
            ot, otb = [], []
            for c in range(KC):
                t, b = A.alloc([128, 512], F32)
                ot.append(t[:])
                otb.append(b[0])
            ots.append(ot)
            otbs.append(otb)
        hin, hinb, sq, sqb, ub, ubb, sq2, sq2b = [], [], [], [], [], [], [], []
        for i in range(3):
            t, b = A.alloc([128, 512], F32)
            hin.append(t)
            hinb.append(b[0])
            t, b = A.alloc([128, 512], BF16)
            sq.append(t)
            sqb.append(b[0])
            t, b = A.alloc([128, 512], BF16)
            sq2.append(t)
            sq2b.append(b[0])
            t, b = A.alloc([128, 512], BF16)
            ub.append(t)
            ubb.append(b[0])
        rstd, rstdb = A.alloc([128, 512], F32)
        rstdb = rstdb[0]
        ssbanks = [6, 5]
        pend = None
        idx = 0
        for qi in range(NQ):
            q0 = qi * QT
            for c in range(KC):
                S.dma("sp", y[:, c, :], yT[c * 128:(c + 1) * 128, q0:q0 + QT],
                      reads=[yTb[(c, t)] for t in range(q0 // 512, (q0 + QT) // 512)], writes=[yb_[c]])
            for i in range(QT // 512):
                tt = q0 // 512 + i
                ot, otb, ssb = ots[idx % 2], otbs[idx % 2], ssbanks[idx % 2]
                if pend is not None:
                    ep_head(*pend)
                dfr = None
                for oc in range(KC):
                    bk = oc % 4
                    for k in range(KC):
                        MM(ps[bk][:], wsb[:, k, oc * 128:(oc + 1) * 128], y[:, k, i * 512:(i + 1) * 512], k == 0, k == KC - 1,
                           [wsbb[oc // 2], yb_[k]], [psb[bk]])
                    if dfr is not None:
                        MM(*dfr)
                    CP("act", ot[oc], ps[bk][:], [psb[bk]], [otb[oc]])
                    s = oc % 3
                    ACT(sq2[s][:], ot[oc], AF.Square, [otb[oc]], [sq2b[s]])
                    dfr = (ps[ssb][:], ones[:], sq2[s][:], oc == 0, oc == KC - 1, [onesb, sq2b[s]], [psb[ssb]])
                    if pend is not None:
                        ep_chunk(oc, *pend)
                MM(*dfr)
                if pend is not None:
                    ep_tail(*pend)
                pend = (tt, ot, otb, ssb, gpost, gnext, hin, hinb, sq, sqb, ub, ubb, rstd, rstdb)
                idx += 1
        epilogue(*pend)
        A.release(m)

    def mlp(l, gpost, gnext):
        m = A.mark()
        NT2 = TM // 512
        hid, hidb = A.alloc([128, 64, TM], BF16, nbuf=64)
        WP = WPool(3)
        mu = A.mark()
        for tp in range(TOKC // TM):
            tk0 = tp * TM
            A.release(mu)
            uu, uub = A.alloc([128, KC, TM], BF16, nbuf=KC)
            rl, rlb = [], []
            for i in range(3):
                t, b = A.alloc([128, 512], F32)
                rl.append(t)
                rlb.append(b[0])
            for c in range(KC):
                S.dma("sp", uu[:, c, :], uT[c * 128:(c + 1) * 128, HALO + tk0:HALO + tk0 + TM],
                      reads=[uTb[(c, 1 + tk0 // 512 + t)] for t in range(NT2)], writes=[uub[c]])
            rli = [0]
            for jb in range(32):
                wv, wb = WP.load(w1[l][:, jb * 256:jb * 256 + 256], KC, 256)
                for j2 in range(2):
                    j = jb * 2 + j2
                    for t in range(NT2):
                        bk = (j2 * NT2 + t) % 4
                        for k in range(KC):
                            MM(ps[bk][:], wv[:, k, j2 * 128:(j2 + 1) * 128], uu[:, k, t * 512:(t + 1) * 512], k == 0, k == KC - 1,
                               [wb, uub[k]], [psb[bk]])
                        s = rli[0] % 3
                        rli[0] += 1
                        ACT(rl[s][:], ps[bk][:], AF.Relu, [psb[bk]], [rlb[s]])
                        TT("dve", hid[:, j, t * 512:(t + 1) * 512], rl[s][:], rl[s][:], ALU.mult, [rlb[s]], [hidb[j]])
            A.release(mu)
            zt, ztb, sq, sqb = [], [], [], []
            for i in range(4):
                t, b = A.alloc([128, 512], F32)
                zt.append(t)
                ztb.append(b[0])
                t, b = A.alloc([128, 512], BF16)
                sq.append(t)
                sqb.append(b[0])
            zi = 0
            dfrB = []
            for ip in range(8):
                for jq in range(4):
                    wv, wb = WP.load(w2[l][jq * 2048:(jq + 1) * 2048, ip * 256:ip * 256 + 256], KC, 256)
                    for i2 in range(2):
                        for t in range(NT2):
                            bk = i2 * NT2 + t
                            for jj in range(KC):
                                j = jq * 16 + jj
                                MM(ps[bk][:], wv[:, jj, i2 * 128:(i2 + 1) * 128], hid[:, j, t * 512:(t + 1) * 512],
                                   jq == 0 and jj == 0, jq == 3 and jj == KC - 1, [wb, hidb[j]], [psb[bk]])
                for d_ in dfrB:
                    MM(*d_)
                dfrB = []
                for i2 in range(2):
                    oc = ip * 2 + i2
                    for t in range(NT2):
                        bk = i2 * NT2 + t
                        s = zi % 4
                        zi += 1
                        CP("act", zt[s][:], ps[bk][:], [psb[bk]], [ztb[s]])
                        ACT(sq[s][:], zt[s][:], AF.Square, [ztb[s]], [sqb[s]])
                        dfrB.append((ps[4 + t][:], ones[:], sq[s][:], oc == 0, oc == KC - 1, [onesb, sqb[s]], [psb[4 + t]]))
                        S.dma("sp", zT[oc * 128:(oc + 1) * 128, t * 512:(t + 1) * 512], zt[s][:], reads=[ztb[s]],
                              writes=[zTb[(oc, t)]])
            for d_ in dfrB:
                MM(*d_)
            A.release(mu)
            ot, otb = [], []
            for c in range(KC):
                t, b = A.alloc([128, 512], F32)
                ot.append(t[:])
                otb.append(b[0])
            hin, hinb, sq, sqb, ub, ubb, osb, osbb = [], [], [], [], [], [], [], []
            for i in range(2):
                t, b = A.alloc([128, 512], F32)
                hin.append(t)
                hinb.append(b[0])
            if gnext is not None:
                for i in range(3):
                    t, b = A.alloc([128, 512], BF16)
                    sq.append(t)
                    sqb.append(b[0])
                    t, b = A.alloc([128, 512], BF16)
                    ub.append(t)
                    ubb.append(b[0])
            rstd, rstdb = A.alloc([128, 512], F32)
            rstdb = rstdb[0]
            if gnext is None:
                for i in range(2):
                    t, b = A.alloc([128, 1024], F32)
                    osb.append(t)
                    osbb.append(b[0])
            for t in range(NT2):
                tt = tk0 // 512 + t
                for c in range(KC):
                    S.dma("sp", ot[c], zT[c * 128:(c + 1) * 128, t * 512:(t + 1) * 512], reads=[zTb[(c, t)]], writes=[otb[c]])
                epilogue(tt, ot, otb, 4 + t, gpost, gnext, hin, hinb, sq, sqb, ub, ubb, rstd, rstdb, osb, osbb)
        A.release(m)

    stage0()
    mixer0(0)
    mixer0(1)
    outproj(w_out0, 16, 32)
    mlp(0, 48, 64 if NL > 1 else None)
    if NL > 1:
        if dbg:
            dbu = nc.dram_tensor("dbg_u", [D, NTOT], F32, kind="ExternalOutput").ap()
            for c in range(KC):
                ob_ = Buf()
                outbs.append(ob_)
                S.dma("pool", dbu[c * 128:(c + 1) * 128, :], uT[c * 128:(c + 1) * 128, :],
                      reads=[uTb[(c, t)] for t in range(NTT + 1)], writes=[ob_])
        if dbg:
            dbh = nc.dram_tensor("dbg_h", [D, TOKC], F32, kind="ExternalOutput").ap()
            for c in range(KC):
                ob_ = Buf()
                outbs.append(ob_)
                S.dma("sp", dbh[c * 128:(c + 1) * 128, :], hT[c * 128:(c + 1) * 128, :],
                      reads=[hTb[(c, t)] for t in range(NTT)], writes=[ob_])
        mixer1(0)
        mixer1(1)
        if dbg:
            dby = nc.dram_tensor("dbg_y", [D, TOKC], F32, kind="ExternalOutput").ap()
            for c in range(KC):
                ob_ = Buf()
                outbs.append(ob_)
                S.dma("pool", dby[c * 128:(c + 1) * 128, :], yT[c * 128:(c + 1) * 128, :],
                      reads=[yTb[(c, t)] for t in range(NTT)], writes=[ob_])
        outproj(w_out1, 80, 96)
        mlp(1, 112, None)
    S.op("sp", lambda e: e.nop(), reads=outbs)
    S.emit()
    return nc


def _consts():
    ident = np.eye(128, dtype=np.float32)
    s = np.arange(128)[:, None]
    t = np.arange(128)[None, :]
    same = (s // 64) == (t // 64)
    U2 = (same & (s <= t)).astype(np.float32)
    R = (same & (s > t)).astype(np.float32)
    return np.concatenate([ident, U2, R], axis=1)


def _bias_tables(rel_bias):
    kpos = (np.arange(5)[:, None] * 128 + np.arange(128)[None, :])
    q = np.arange(128)
    rel = (512 + q)[None, None, :] - kpos[:, :, None]
    idx = np.clip(rel, -63, 256) + 63
    qc = (512 + q) // 64
    kc = kpos // 64
    valid = (kc[:, :, None] <= qc[None, None, :]) & (kc[:, :, None] >= qc[None, None, :] - 8)
    out = np.empty((8, 128, 640), np.float32)
    for h in range(8):
        b = rel_bias[h][idx]
        b = np.where(valid, b, np.float32(NEG))
        out[h] = b.transpose(1, 0, 2).reshape(128, 640)
    return out


def _prep_shared(inp, NL):
    f = np.float32
    norm_g = np.asarray(inp["norm_g"], f)
    sh = {}
    sh["w_in0"] = np.ascontiguousarray(np.asarray(inp["even_w_in"], f)[0])
    sh["w_out0"] = np.ascontiguousarray(np.asarray(inp["even_w_out"], f)[0])
    sh["w_in1"] = np.ascontiguousarray(np.asarray(inp["odd_w_in"], f)[0])
    sh["w_out1"] = np.ascontiguousarray(np.asarray(inp["odd_w_out"], f)[0])
    for l in range(2):
        sh["w1_%d" % l] = np.ascontiguousarray(np.asarray(inp["mlp_w1"], f)[l])
        sh["w2_%d" % l] = np.ascontiguousarray(np.asarray(inp["mlp_w2"], f)[l])
    sh["gT"] = np.ascontiguousarray(norm_g.reshape(8, 16, 128).transpose(2, 0, 1).reshape(128, 128))
    sh["convT"] = np.ascontiguousarray(np.asarray(inp["even_conv_w"], f)[0].reshape(3, 8, 128).transpose(2, 0, 1).reshape(128, 24))
    sh["biasT"] = _bias_tables(np.asarray(inp["even_rel_bias"], f)[0])
    sh["cst"] = _consts()
    lb = np.asarray(inp["hgrn_lb"], f)
    sh["lbT"] = np.ascontiguousarray(lb.reshape(2, 8, 128).transpose(2, 0, 1).reshape(128, 16))
    sh["lbrow"] = np.ascontiguousarray(np.broadcast_to(lb.reshape(1, 2048), (128, 2048)))
    sh["hngT"] = np.ascontiguousarray(np.asarray(inp["hgrn_norm_g"], f)[0].reshape(8, 128).T)
    sh["gngT"] = np.ascontiguousarray(np.asarray(inp["gla_norm_g"], f)[0].reshape(8, 128).T)
    sh["wa2"] = np.ascontiguousarray(np.asarray(inp["gla_wa2"], f)[0])
    sh["barow"] = np.ascontiguousarray(np.broadcast_to(np.asarray(inp["gla_ba"], f)[0].reshape(1, 512), (128, 512)))
    return sh


def run(inp, NL=2, tokc=None, trace=False, dbg=False):
    x = np.asarray(inp["x"], np.float32)
    B, SQ, _ = x.shape
    tokc = tokc or SQ
    nc = build(tokc, NL, dbg)
    sh = _prep_shared(inp, NL)
    in_maps = []
    for b in range(B):
        m = dict(sh)
        m["x"] = np.ascontiguousarray(x[b, :tokc])
        in_maps.append(m)
    res = run_bass_kernel_spmd(nc, in_maps, core_ids=list(range(B)), trace=trace)
    outs = np.stack([np.asarray(r["out"]) for r in res.results], axis=0)
    if dbg:
        return outs.astype(np.float32), res, [(np.asarray(r["dbg_y"]), np.asarray(r["dbg_u"]), np.asarray(r["dbg_h"])) for r in res.results]
    return outs.astype(np.float32), res


def kernel(**inputs):
    o, _ = run(inputs)
    return o
```

```python
import contextlib
import numpy as np
import concourse.bass as bass
import concourse.mybir as mybir
from concourse.bass_utils import run_bass_kernel_spmd

F32 = mybir.dt.float32
BF16 = mybir.dt.bfloat16
ALU = mybir.AluOpType
AF = mybir.ActivationFunctionType

N_DMA_SEMS = 40
import os as _os0
STRICT_SAME_ENGINE = bool(int(_os0.environ.get("STRICT_SE", "1")))
D = 2048
KC = 16
DFF = 8192
HALO = 512
EPS = 1e-6
NEG = -30000.0


class Buf:
    __slots__ = ("w", "rs", "excl")

    def __init__(self, excl=False):
        self.w = None
        self.rs = []
        self.excl = excl


class Op:
    __slots__ = ("eng", "fn", "seq", "waits", "sig", "tick", "dma", "dsem", "dval", "vc", "dwaits")


class Sched:
    ENGS = ("pe", "act", "dve", "pool", "sp")

    def __init__(self, nc):
        self.nc = nc
        self.ops = {e: [] for e in self.ENGS}
        self.known = {e: {f: -1 for f in self.ENGS} for e in self.ENGS}
        self.known_dma = {e: {} for e in self.ENGS}
        self.dma_last = [None] * N_DMA_SEMS
        self.dma_cnt = [0] * N_DMA_SEMS
        self.dma_pool = {"sp": list(range(0, 24)), "pool": list(range(24, N_DMA_SEMS)), "act": []}
        self.dma_rr = {"sp": 0, "pool": 0, "act": 0}

    def _add(self, eng, fn, reads, writes, dma):
        o = Op()
        o.eng = eng
        o.fn = fn
        o.dma = dma
        o.sig = False
        o.tick = 0
        o.waits = []
        o.dwaits = []
        o.seq = len(self.ops[eng])
        kn = self.known[eng]
        kd = self.known_dma[eng]
        deps = []
        for r in reads:
            if r.w is not None:
                deps.append((r.w, True))
            if r.excl:
                for rr in r.rs:
                    if rr.eng != eng:
                        deps.append((rr, False))
        for w in writes:
            if w.w is not None:
                deps.append((w.w, False))
            for rr in w.rs:
                deps.append((rr, False))
        for d, raw in deps:
            if d.dma:
                if kd.get(d.dsem, 0) < d.dval:
                    kd[d.dsem] = d.dval
                    o.dwaits.append((d.dsem, d.dval))
            else:
                if d.eng == eng and not dma:
                    if eng == "pe" or (not raw and not STRICT_SAME_ENGINE):
                        continue
                if kn[d.eng] >= d.seq:
                    continue
                kn[d.eng] = d.seq
                for f, s in d.vc.items():
                    if f != eng and kn[f] < s:
                        kn[f] = s
                d.sig = True
                o.waits.append(d)
        if dma:
            pool_ = self.dma_pool[eng]
            k = pool_[self.dma_rr[eng] % len(pool_)]
            self.dma_rr[eng] += 1
            prev = self.dma_last[k]
            if prev is not None and kd.get(k, 0) < prev.dval:
                kd[k] = prev.dval
                o.dwaits.append((k, prev.dval))
            self.dma_cnt[k] += 16
            o.dsem = k
            o.dval = self.dma_cnt[k]
            self.dma_last[k] = o
        best = {}
        for d in o.waits:
            if d.eng not in best or best[d.eng].seq < d.seq:
                best[d.eng] = d
        o.waits = list(best.values())
        o.vc = dict(kn)
        o.vc[eng] = o.seq - 1 if dma else o.seq
        for r in reads:
            r.rs.append(o)
        for w in writes:
            w.w = o
            w.rs = []
        self.ops[eng].append(o)
        return o

    def op(self, eng, fn, reads=(), writes=()):
        return self._add(eng, fn, reads, writes, False)

    def dma(self, q, out, in_, reads=(), writes=()):
        return self._add(q, lambda e: e.dma_start(out=out, in_=in_), reads, writes, True)

    def emit(self):
        nc = self.nc
        with contextlib.ExitStack() as st:
            esem = {e: st.enter_context(nc.semaphore("s_" + e)) for e in self.ENGS}
            dsem = [st.enter_context(nc.semaphore("d%d" % i)) for i in range(N_DMA_SEMS)]
            for e in self.ENGS:
                t = 0
                for o in self.ops[e]:
                    if o.sig and not o.dma:
                        t += 1
                        o.tick = t
            block = st.enter_context(nc.Block())

            def run(ename):
                def body(eng):
                    s_own = esem[ename]
                    for o in self.ops[ename]:
                        for d in o.waits:
                            eng.wait_ge(esem[d.eng], d.tick)
                        for (k, v) in o.dwaits:
                            eng.wait_ge(dsem[k], v)
                        ins = o.fn(eng)
                        if o.dma:
                            ins.then_inc(dsem[o.dsem], 16)
                        elif o.sig:
                            ins.then_inc(s_own, 1)
                return body

            block.tensor(run("pe"))
            block.scalar(run("act"))
            block.vector(run("dve"))
            block.gpsimd(run("pool"))
            block.sync(run("sp"))


class Arena:
    def __init__(self, nc, lo, hi):
        self.nc = nc
        self.lo = lo
        self.hi = hi
        self.top = lo
        self.dead = []
        self.live = []
        self.n = 0

    def alloc(self, shape, dtype, nbuf=1):
        nb = 4 if dtype == F32 else 2
        size = nb
        for s in shape[1:]:
            size *= s
        size = (size + 63) // 64 * 64
        off = self.top
        self.top += size
        assert self.top <= self.hi, ("SBUF arena overflow", self.top, self.hi)
        self.n += 1
        t = self.nc.alloc_sbuf_tensor_at("t%d" % self.n, list(shape), dtype, offset=off)
        bufs = [Buf() for _ in range(nbuf)]
        acc = []
        keep = []
        for (lo, hi, bl) in self.dead:
            if lo < off + size and off < hi:
                for b in bl:
                    if b.w is not None:
                        acc.append(b.w)
                    acc.extend(b.rs)
                if not (off <= lo and hi <= off + size):
                    keep.append((lo, hi, bl))
            else:
                keep.append((lo, hi, bl))
        self.dead = keep
        for b in bufs:
            b.rs = list(acc)
        self.live.append((off, off + size, bufs))
        return t, bufs

    def mark(self):
        return (self.top, len(self.live))

    def release(self, m):
        top, n = m
        self.dead.extend(self.live[n:])
        del self.live[n:]
        self.top = top


def build(TOKC, NL=2, dbg=False):
    nc = bass.Bass("TRN2", target_bir_lowering=False)
    NTT = TOKC // 512
    HT = TOKC // 2
    HTT = HT // 512
    TM = min(1024, TOKC)
    NTOT = HALO + TOKC

    def din(name, shape, dt=F32):
        return nc.dram_tensor(name, list(shape), dt, kind="ExternalInput").ap()

    x = din("x", [TOKC, D])
    w_in0 = din("w_in0", [D, 6144])
    w_out0 = din("w_out0", [D, D])
    w_in1 = din("w_in1", [D, 7184])
    w_out1 = din("w_out1", [D, D])
    w1 = [din("w1_%d" % l, [D, DFF]) for l in range(2)]
    w2 = [din("w2_%d" % l, [DFF, D]) for l in range(2)]
    gT_d = din("gT", [128, 128])
    convT_d = din("convT", [128, 24])
    biasT_d = din("biasT", [8, 128, 640])
    cst_d = din("cst", [128, 3 * 128])
    lbT_d = din("lbT", [128, 16])
    lbrow_d = din("lbrow", [128, 2048])
    hngT_d = din("hngT", [128, 8])
    gngT_d = din("gngT", [128, 8])
    wa2_d = din("wa2", [16, 512])
    barow_d = din("barow", [128, 512])
    out = nc.dram_tensor("out", [TOKC, D], F32, kind="ExternalOutput").ap()

    hT = nc.dram_tensor("hT", [D, TOKC], F32).ap()
    uT = nc.dram_tensor("uT", [D, NTOT], BF16).ap()
    yT = nc.dram_tensor("yT", [D, TOKC], BF16).ap()
    zT = nc.dram_tensor("zT", [D, TM], F32).ap()
    hTb = {(c, t): Buf() for c in range(KC) for t in range(NTT)}
    uTb = {(c, t): Buf() for c in range(KC) for t in range(NTT + 1)}
    yTb = {(c, t): Buf() for c in range(KC) for t in range(NTT)}
    zTb = {(c, t): Buf() for c in range(KC) for t in range(TM // 512)}
    outbs = []
    dbgs = {}

    S = Sched(nc)
    A = Arena(nc, 16640, 229376)

    def MM(o, l, r, start, stop, reads, writes):
        S.op("pe", lambda e: e.matmul(o, l, r, start=start, stop=stop), reads, writes)

    def TR(o, i, ident, reads, writes):
        S.op("pe", lambda e: e.transpose(o, i, ident), reads, writes)

    def ACT(o, i, func, reads, writes, bias=None, scale=None):
        kw = {}
        if bias is not None:
            kw["bias"] = bias
        if scale is not None:
            kw["scale"] = scale
        S.op("act", lambda e: e.activation(o, i, func, **kw), reads, writes)

    def TT(eng, o, a, b, op, reads, writes):
        S.op(eng, lambda e: e.tensor_tensor(o, a, b, op), reads, writes)

    def TS(eng, o, a, s1, s2, op0, op1, reads, writes):
        if op1 is None:
            S.op(eng, lambda e: e.tensor_scalar(o, a, s1, None, op0), reads, writes)
        else:
            S.op(eng, lambda e: e.tensor_scalar(o, a, s1, s2, op0, op1), reads, writes)

    def STT(eng, o, a, s, b, op0, op1, reads, writes):
        assert eng == "dve"
        S.op(eng, lambda e: e.scalar_tensor_tensor(o, a, s, b, op0, op1), reads, writes)

    def CP(eng, o, i, reads, writes):
        if eng == "act":
            S.op("act", lambda e: e.copy(o, i), reads, writes)
        else:
            S.op(eng, lambda e: e.tensor_copy(o, i), reads, writes)

    def RCP(o, i, reads, writes):
        S.op("dve", lambda e: e.reciprocal(o, i), reads, writes)

    def MSET(eng, o, v, writes):
        S.op(eng, lambda e: e.memset(o, v), (), writes)

    ps = []
    psb = []
    for i in range(8):
        ps.append(nc.alloc_psum_tensor("ps%d" % i, [128, 512], F32))
        psb.append(Buf(excl=True))

    cst, cstb = A.alloc([128, 384], F32)
    cstb = cstb[0]
    S.dma("sp", cst[:], cst_d[:, :], writes=[cstb])
    ident = cst[:, 0:128]
    U2 = cst[:, 128:256]
    Rm = cst[:, 256:384]
    ones, onesb = A.alloc([128, 128], BF16)
    onesb = onesb[0]
    MSET("dve", ones[:], 1.0, [onesb])
    gT, gTb = A.alloc([128, 128], F32)
    gTb = gTb[0]
    S.dma("sp", gT[:], gT_d[:, :], writes=[gTb])
    convT, convTb = A.alloc([128, 24], F32)
    convTb = convTb[0]
    S.dma("sp", convT[:], convT_d[:, :], writes=[convTb])
    zbf, zbfb = A.alloc([128, 512], BF16)
    zbfb = zbfb[0]
    MSET("dve", zbf[:], 0.0, [zbfb])
    negc, negcb = A.alloc([128, 1], F32)
    negcb = negcb[0]
    MSET("dve", negc[:], NEG, [negcb])

    for c in range(KC):
        S.dma("sp", uT[c * 128:(c + 1) * 128, 0:HALO], zbf[:, 0:HALO], reads=[zbfb], writes=[uTb[(c, 0)]])

    class WPool:
        def __init__(self, n, elems=4096):
            self.t = []
            self.prev = []
            for i in range(n):
                t, b = A.alloc([128, elems], BF16)
                self.t.append(t)
                self.prev.append(b)
            self.i = 0

        def load_multi(self, blocks, K):
            i = self.i
            self.i = (i + 1) % len(self.t)
            tot = sum(n for _, n in blocks)
            v = self.t[i][:, 0:K * tot].rearrange("p (k n) -> p k n", k=K)
            haz = []
            for b in self.prev[i]:
                if b.w is not None:
                    haz.append(b.w)
                haz.extend(b.rs)
            bufs = []
            c0 = 0
            for src, n in blocks:
                b = Buf()
                b.rs = list(haz)
                S.dma("pool", v[:, :, c0:c0 + n], src.rearrange("(k p) n -> p k n", p=128), writes=[b])
                bufs.append(b)
                c0 += n
            self.prev[i][:] = bufs
            return v, bufs

        def load(self, src_ap, K, ncols):
            v, bufs = self.load_multi([(src_ap, ncols)], K)
            return v, bufs[0]

    eng_rr = [0]

    def alt(*engs):
        eng_rr[0] += 1
        return engs[eng_rr[0] % len(engs)]

    UQ = ["pool"]

    def norm_sq(ht, htb, c, sq, sqb):
        s = c % len(sq)
        ACT(sq[s][:], ht[c], AF.Square, [htb[c]], [sqb[s]])
        MM(ps[7][:], ones[:], sq[s][:], c == 0, c == KC - 1, [onesb, sqb[s]], [psb[7]])

    def norm_to_u(ht, htb, gcol, tt_u, sq, sqb, ub, ubb, rstd, rstdb, presq=False):
        ssb = 7
        if not presq:
            for c in range(KC):
                norm_sq(ht, htb, c, sq, sqb)
        ACT(rstd[:], ps[ssb][:], AF.Ln, [psb[ssb]], [rstdb], bias=EPS, scale=1.0 / D)
        ACT(rstd[:], rstd[:], AF.Exp, [rstdb], [rstdb], scale=-0.5)
        for c in range(KC):
            s = c % len(ub)
            STT("dve", ub[s][:], ht[c], gT[:, gcol + c:gcol + c + 1], rstd[:], ALU.mult, ALU.mult,
                [htb[c], gTb, rstdb], [ubb[s]])
            S.dma(UQ[0], uT[c * 128:(c + 1) * 128, tt_u * 512:(tt_u + 1) * 512], ub[s][:],
                  reads=[ubb[s]], writes=[uTb[(c, tt_u)]])

    def ep_head(tt, ot, otb, ssbank, gpost, gnext, hin, hinb, sq, sqb, ub, ubb, rstd, rstdb, osb=None, osbb=None):
        ACT(rstd[:], ps[ssbank][:], AF.Ln, [psb[ssbank]], [rstdb], bias=EPS, scale=1.0 / D)
        ACT(rstd[:], rstd[:], AF.Exp, [rstdb], [rstdb], scale=-0.5)

    def ep_chunk(c, tt, ot, otb, ssbank, gpost, gnext, hin, hinb, sq, sqb, ub, ubb, rstd, rstdb, osb=None, osbb=None):
        s = c % len(hin)
        S.dma("sp", hin[s][:], hT[c * 128:(c + 1) * 128, tt * 512:(tt + 1) * 512],
              reads=[hTb[(c, tt)]], writes=[hinb[s]])
        STT("dve", ot[c], ot[c], gT[:, gpost + c:gpost + c + 1], rstd[:], ALU.mult, ALU.mult,
            [otb[c], gTb, rstdb], [otb[c]])
        TT("dve", ot[c], ot[c], hin[s][:], ALU.add, [otb[c], hinb[s]], [otb[c]])
        if gnext is not None:
            norm_sq(ot, otb, c, sq, sqb)

    def epilogue(tt, ot, otb, ssbank, gpost, gnext, hin, hinb, sq, sqb, ub, ubb, rstd, rstdb, osb=None, osbb=None):
        args = (tt, ot, otb, ssbank, gpost, gnext, hin, hinb, sq, sqb, ub, ubb, rstd, rstdb, osb, osbb)
        ep_head(*args)
        for c in range(KC):
            ep_chunk(c, *args)
        ep_tail(*args)

    def ep_tail(tt, ot, otb, ssbank, gpost, gnext, hin, hinb, sq, sqb, ub, ubb, rstd, rstdb, osb=None, osbb=None):
        if gnext is not None:
            for c in range(KC):
                S.dma("sp", hT[c * 128:(c + 1) * 128, tt * 512:(tt + 1) * 512], ot[c],
                      reads=[otb[c]], writes=[hTb[(c, tt)]])
            norm_to_u(ot, otb, gnext, tt + 1, sq, sqb, ub, ubb, rstd, rstdb, presq=True)
        else:
            for tb in range(4):
                r0 = tt * 512 + tb * 128
                for ch in range(2):
                    s = ch
                    for c2 in range(2):
                        c4 = ch * 2 + c2
                        bank = c4 % 2
                        for cc in range(4):
                            c = c4 * 4 + cc
                            TR(ps[bank][:, cc * 128:(cc + 1) * 128], ot[c][:, tb * 128:(tb + 1) * 128], ident,
                               [otb[c], cstb], [psb[bank]])
                        CP(alt("act", "dve"), osb[s][:, c2 * 512:(c2 + 1) * 512], ps[bank][:], [psb[bank]], [osbb[s]])
                    ob_ = Buf()
                    outbs.append(ob_)
                    S.dma("sp", out[r0:r0 + 128, ch * 1024:(ch + 1) * 1024], osb[s][:], reads=[osbb[s]], writes=[ob_])

    def stage0():
        m = A.mark()
        xs = []
        xsb = []
        for i in range(2):
            t, b = A.alloc([128, 4, D], F32, nbuf=4)
            xs.append(t)
            xsb.append(b)
        ht = []
        htb = []
        for c in range(KC):
            t, b = A.alloc([128, 512], F32)
            ht.append(t[:])
            htb.append(b[0])
        sq, sqb, ub, ubb = [], [], [], []
        for i in range(3):
            t, b = A.alloc([128, 512], BF16)
            sq.append(t)
            sqb.append(b[0])
            t, b = A.alloc([128, 512], BF16)
            ub.append(t)
            ubb.append(b[0])
        rstd, rstdb = A.alloc([128, 512], F32)
        rstdb = rstdb[0]
        for tt in range(NTT):
            s = tt % 2
            for tb in range(4):
                r0 = (tt * 4 + tb) * 128
                S.dma("sp", xs[s][:, tb, :], x[r0:r0 + 128, :], writes=[xsb[s][tb]])
            for c in range(KC):
                bank = c % 4
                for tb in range(4):
                    TR(ps[bank][:, tb * 128:(tb + 1) * 128], xs[s][:, tb, c * 128:(c + 1) * 128], ident,
                       [xsb[s][tb], cstb], [psb[bank]])
                CP(alt("act", "dve"), ht[c], ps[bank][:], [psb[bank]], [htb[c]])
                S.dma("pool", hT[c * 128:(c + 1) * 128, tt * 512:(tt + 1) * 512], ht[c],
                      reads=[htb[c]], writes=[hTb[(c, tt)]])
            norm_to_u(ht, htb, 0, tt + 1, sq, sqb, ub, ubb, rstd, rstdb)
        A.release(m)

    def mixer0(half):
        m = A.mark()
        t0 = half * HT
        NH = HALO + HT
        u, ub_ = A.alloc([128, KC, NH], BF16, nbuf=KC)
        for c in range(KC):
            rd = [uTb[(c, t)] for t in range(t0 // 512, t0 // 512 + NH // 512)]
            S.dma("sp", u[:, c, :], uT[c * 128:(c + 1) * 128, t0:t0 + NH], reads=rd, writes=[ub_[c]])
        WP = WPool(4)
        ybf = []
        ybfb = []
        for i in range(2):
            t, b = A.alloc([128, HT], BF16)
            ybf.append(t)
            ybfb.append(b[0])
        yi = [0]

        def proj_fm(wv, wb, col, tok0, ntok, bank):
            for k in range(KC):
                MM(ps[bank][:, 0:ntok], wv[:, k, col:col + 128], u[:, k, tok0:tok0 + ntok], k == 0, k == KC - 1,
                   [wb, ub_[k]], [psb[bank]])

        mA = A.mark()
        NCV = HT + 128
        ct, ctb = A.alloc([128, NCV], F32)
        ctb = ctb[0]
        ucv, ucvb = A.alloc([128, NCV], F32)
        ucvb = ucvb[0]
        acc, accb = A.alloc([128, HT], F32)
        accb = accb[0]
        for gp in range(4):
            wbv, wbb = WP.load(w_in0[:, gp * 256:gp * 256 + 256], KC, 256)
            wcv, wcb = WP.load(w_in0[:, 1024 + gp * 256:1024 + gp * 256 + 256], KC, 256)
            whv, whb = WP.load(w_in0[:, 2048 + gp * 256:2048 + gp * 256 + 256], KC, 256)
            for g2 in range(2):
                g = gp * 2 + g2
                col = g2 * 128
                pieces = [(HALO - 128, 128)] + [(HALO + i * 512, 512) for i in range(HTT)]
                for pi, (tk, n) in enumerate(pieces):
                    o0 = tk - (HALO - 128)
                    bk = pi % 2
                    proj_fm(wcv, wcb, col, tk, n, bk)
                    CP("act", ct[:, o0:o0 + n], ps[bk][:, 0:n], [psb[bk]], [ctb])
                    bk2 = 2 + pi % 2
                    proj_fm(whv, whb, col, tk, n, bk2)
                    TT("dve", ucv[:, o0:o0 + n], ps[bk2][:, 0:n], ct[:, o0:o0 + n], ALU.mult, [psb[bk2], ctb], [ucvb])
                TS("dve", acc[:], ucv[:, 126:126 + HT], convT[:, g:g + 1], None, ALU.mult, None, [ucvb, convTb], [accb])
                STT("dve", acc[:], ucv[:, 127:127 + HT], convT[:, 8 + g:9 + g], acc[:], ALU.mult, ALU.add,
                    [ucvb, convTb, accb], [accb])
                STT("dve", acc[:], ucv[:, 128:128 + HT], convT[:, 16 + g:17 + g], acc[:], ALU.mult, ALU.add,
                    [ucvb, convTb, accb], [accb])
                ys = yi[0] % 2
                yi[0] += 1
                for i in range(HTT):
                    bk = 4 + i % 2
                    proj_fm(wbv, wbb, col, HALO + i * 512, 512, bk)
                    TT("dve", ybf[ys][:, i * 512:(i + 1) * 512], ps[bk][:], acc[:, i * 512:(i + 1) * 512], ALU.mult,
                       [psb[bk], accb], [ybfb[ys]])
                S.dma("sp", yT[g * 128:(g + 1) * 128, t0:t0 + HT], ybf[ys][:], reads=[ybfb[ys]],
                      writes=[yTb[(g, t)] for t in range(t0 // 512, t0 // 512 + HTT)])
        A.release(mA)

        NKB = NH // 128
        qT, qTb = A.alloc([128, HT], BF16)
        qTb = qTb[0]
        kT, kTb = A.alloc([128, NH], BF16)
        kTb = kTb[0]
        vt, vtb = A.alloc([128, NKB, 256], BF16)
        vtb = vtb[0]
        bT = []
        bTb = []
        for i in range(2):
            t, b = A.alloc([128, 640], F32)
            bT.append(t)
            bTb.append(b[0])
        et, etb, pT, pTb, rd_, rdb = [], [], [], [], [], []
        for i in range(2):
            t, b = A.alloc([128, 640], F32)
            et.append(t)
            etb.append(b[0])
            t, b = A.alloc([128, 640], BF16)
            pT.append(t)
            pTb.append(b[0])
            t, b = A.alloc([128, 128], F32)
            rd_.append(t)
            rdb.append(b[0])
        scale = 128.0 ** -0.5
        for hp in range(4):
            wqv, wqb = WP.load(w_in0[:, 3072 + hp * 256:3072 + hp * 256 + 256], KC, 256)
            wkv, wkb = WP.load(w_in0[:, 4096 + hp * 256:4096 + hp * 256 + 256], KC, 256)
            wvv, wvb = WP.load(w_in0[:, 5120 + hp * 256:5120 + hp * 256 + 256], KC, 256)
            for tb in range(NKB):
                bk = tb % 2
                for k in range(KC):
                    MM(ps[bk][:, 0:256], u[:, k, tb * 128:(tb + 1) * 128], wvv[:, k, :], k == 0, k == KC - 1,
                       [ub_[k], wvb], [psb[bk]])
                CP(alt("act", "dve"), vt[:, tb, :], ps[bk][:, 0:256], [psb[bk]], [vtb])
            for h2 in range(2):
                hd = hp * 2 + h2
                col = h2 * 128
                bs = hd % 2
                S.dma("sp", bT[bs][:], biasT_d[hd], writes=[bTb[bs]])
                for i in range(HTT):
                    bk = 2 + i % 2
                    proj_fm(wqv, wqb, col, HALO + i * 512, 512, bk)
                    ACT(qT[:, i * 512:(i + 1) * 512], ps[bk][:], AF.Copy, [psb[bk]], [qTb], scale=scale)
                for i in range(NH // 512):
                    bk = 2 + i % 2
                    proj_fm(wkv, wkb, col, i * 512, 512, bk)
                    CP(alt("act", "dve"), kT[:, i * 512:(i + 1) * 512], ps[bk][:], [psb[bk]], [kTb])
                ys = yi[0] % 2
                yi[0] += 1
                for qt in range(HT // 128):
                    s = qt % 2
                    bx = 4 - 2 * (qt % 2)
                    bo = 6 + qt % 2
                    for kb in range(5):
                        dst = ps[bx][:, kb * 128:(kb + 1) * 128] if kb < 4 else ps[bx + 1][:, 0:128]
                        dbuf = psb[bx] if kb < 4 else psb[bx + 1]
                        MM(dst, kT[:, (qt + kb) * 128:(qt + kb + 1) * 128], qT[:, qt * 128:(qt + 1) * 128], True, True,
                           [kTb, qTb], [dbuf])
                    TT("dve", et[s][:, 0:512], ps[bx][:], bT[bs][:, 0:512], ALU.add, [psb[bx], bTb[bs]], [etb[s]])
                    TT("dve", et[s][:, 512:640], ps[bx + 1][:, 0:128], bT[bs][:, 512:640], ALU.add,
                       [psb[bx + 1], bTb[bs]], [etb[s]])
                    nh = max(0, 4 - qt) if half == 0 else 0
                    if nh > 0:
                        ACT(pT[s][:, 0:nh * 128], et[s][:, 0:nh * 128], AF.Exp, [etb[s], negcb], [pTb[s]], bias=negc[:])
                    ACT(pT[s][:, nh * 128:640], et[s][:, nh * 128:640], AF.Exp, [etb[s]], [pTb[s]])
                    for kb in range(5):
                        MM(ps[bo][:, 0:128], vt[:, qt + kb, col:col + 128], pT[s][:, kb * 128:(kb + 1) * 128], kb == 0, kb == 4,
                           [vtb, pTb[s]], [psb[bo]])
                    for kb in range(5):
                        MM(ps[bo][:, 128:256], ones[:], pT[s][:, kb * 128:(kb + 1) * 128], kb == 0, kb == 4,
                           [onesb, pTb[s]], [psb[bo]])
                    RCP(rd_[s][:], ps[bo][:, 128:256], [psb[bo]], [rdb[s]])
                    TT("dve", ybf[ys][:, qt * 128:(qt + 1) * 128], ps[bo][:, 0:128], rd_[s][:], ALU.mult,
                       [psb[bo], rdb[s]], [ybfb[ys]])
                c = 8 + hd
                S.dma("sp", yT[c * 128:(c + 1) * 128, t0:t0 + HT], ybf[ys][:], reads=[ybfb[ys]],
                      writes=[yTb[(c, t)] for t in range(t0 // 512, t0 // 512 + HTT)])
        A.release(m)


    stS = nc.dram_tensor("stS", [128, 12 * 256], F32).ap()
    stSb = [Buf() for _ in range(12)]

    def ring(shape, dtype, n=2):
        ts, bs = [], []
        for i in range(n):
            t, b = A.alloc(shape, dtype)
            ts.append(t)
            bs.append(b[0])
        return ts, bs

    def mixer1(half):
        m = A.mark()
        t0 = half * HT
        u, ub_ = A.alloc([128, KC, HT], BF16, nbuf=KC)
        for c in range(KC):
            S.dma("sp", u[:, c, :], uT[c * 128:(c + 1) * 128, HALO + t0:HALO + t0 + HT],
                  reads=[uTb[(c, 1 + t0 // 512 + t)] for t in range(HTT)], writes=[ub_[c]])
        WP = WPool(3, 8192)
        lbr, lbrb = A.alloc([128, 1024], F32)
        lbrb = lbrb[0]
        omlr, omlrb = A.alloc([128, 1024], F32)
        omlrb = omlrb[0]
        lbc, lbcb = A.alloc([128, 8], F32)
        lbcb = lbcb[0]
        omlc, omlcb = A.alloc([128, 8], F32)
        omlcb = omlcb[0]
        nomlc, nomlcb = A.alloc([128, 8], F32)
        nomlcb = nomlcb[0]
        mt = A.mark()
        lr, lrb = A.alloc([128, 2048], F32)
        lrb = lrb[0]
        lc, lcb = A.alloc([128, 16], F32)
        lcb = lcb[0]
        S.dma("sp", lr[:], lbrow_d[:, :], writes=[lrb])
        S.dma("sp", lc[:], lbT_d[:, :], writes=[lcb])
        TT("dve", lbr[:], lr[:, 1024:2048], lr[:, 0:1024], ALU.subtract, [lrb], [lbrb])
        ACT(lbr[:], lbr[:], AF.Exp, [lbrb], [lbrb], scale=-1.0)
        TS("pool", lbr[:], lbr[:], 1.0, None, ALU.add, None, [lbrb], [lbrb])
        RCP(lbr[:], lbr[:], [lbrb], [lbrb])
        TS("pool", omlr[:], lbr[:], -1.0, 1.0, ALU.mult, ALU.add, [lbrb], [omlrb])
        TT("dve", lbc[:], lc[:, 8:16], lc[:, 0:8], ALU.subtract, [lcb], [lbcb])
        ACT(lbc[:], lbc[:], AF.Exp, [lbcb], [lbcb], scale=-1.0)
        TS("pool", lbc[:], lbc[:], 1.0, None, ALU.add, None, [lbcb], [lbcb])
        RCP(lbc[:], lbc[:], [lbcb], [lbcb])
        TS("pool", omlc[:], lbc[:], -1.0, 1.0, ALU.mult, ALU.add, [lbcb], [omlcb])
        TS("pool", nomlc[:], lbc[:], -1.0, None, ALU.add, None, [lbcb], [nomlcb])
        A.release(mt)
        wa2s, wa2sb = A.alloc([16, 512], F32)
        wa2sb = wa2sb[0]
        S.dma("sp", wa2s[:], wa2_d[:, :], writes=[wa2sb])
        barow, barowb = A.alloc([128, 512], F32)
        barowb = barowb[0]
        S.dma("sp", barow[:], barow_d[:, :], writes=[barowb])
        hng, hngb = A.alloc([128, 8], F32)
        hngb = hngb[0]
        S.dma("sp", hng[:], hngT_d[:, :], writes=[hngb])
        gng, gngb = A.alloc([128, 8], F32)
        gngb = gngb[0]
        S.dma("sp", gng[:], gngT_d[:, :], writes=[gngb])
        gaT, gaTb = A.alloc([16, HT], F32)
        gaTb = gaTb[0]
        wga, wgab = WP.load(w_in1[:, 7168:7184], KC, 16)
        for g in range(HTT):
            for k in range(KC):
                MM(ps[7][0:16, :], wga[:, k, 0:16], u[:, k, g * 512:(g + 1) * 512], k == 0, k == KC - 1,
                   [wgab, ub_[k]], [psb[7]])
            CP("act", gaT[0:16, g * 512:(g + 1) * 512], ps[7][0:16, :], [psb[7]], [gaTb])
        qs, qsb = ring([128, 512], F32, 2)
        kf, kfb = ring([128, 512], F32, 2)
        sg4, sg4b = ring([128, 512], F32, 2)
        tA, tAb = ring([128, 512], F32, 1)
        gs0, gs0b = ring([128, 512], F32, 1)
        gs1, gs1b = ring([128, 512], F32, 1)
        oh0, oh0b = ring([128, 512], F32, 1)
        oh1, oh1b = ring([128, 512], F32, 1)
        sqr, sqrb = ring([128, 512], BF16)
        rs, rsb = ring([128, 512], F32, 1)
        yb0, yb0b = ring([128, 512], BF16)
        yb1, yb1b = ring([128, 512], BF16)
        X1, X1b = ring([128, 512], F32, 1)
        X2, X2b = ring([128, 512], F32, 1)
        la4, la4b = ring([128, 512], F32, 1)
        kt4, kt4b = ring([128, 512], F32, 2)
        kh4, kh4b = ring([128, 512], BF16, 1)
        ktl4, ktl4b = ring([128, 512], BF16, 1)
        vt4, vt4b = ring([128, 4, 256], BF16, 3)
        qtl4, qtl4b = ring([128, 512], BF16)
        AT4, AT4b = ring([128, 512], BF16)
        c4a, c4ab = ring([128, 512], F32, 1)
        c4b, c4bb = ring([128, 512], F32, 1)
        U4, U4b = ring([128, 512], F32, 1)
        for j in range(4):
            CP("act", U4[0][:, j * 128:(j + 1) * 128], U2, [cstb], [U4b[0]])
        S32, S32b = ring([128, 256], F32)
        Sring, Sringb = ring([128, 256], BF16, 18)
        qscale = 128.0 ** -0.5

        def silu_from_psum(bank, dst, dstb):
            ACT(tA[0][:], ps[bank][:], AF.Sigmoid, [psb[bank]], [tAb[0]])
            TT("dve", dst, ps[bank][:], tA[0][:], ALU.mult, [psb[bank], tAb[0]], [dstb])

        heads = [("h", i) for i in range(8)] + [("g", i) for i in range(4)]
        import os as _os
        if _os.environ.get("DBG_HEADS"):
            heads = [h for h in heads if h[0] in _os.environ["DBG_HEADS"]][:int(_os.environ.get("DBG_NH", "12"))]
        scount = [0]
        for hidx, (kind, hi) in enumerate(heads):
            dv = 128 if kind == "h" else 256
            NE = dv // 128
            if kind == "h":
                tokb = [(w_in1[:, 1024 + hi * 128:1024 + hi * 128 + 128], 128),
                        (w_in1[:, 2048 + hi * 128:2048 + hi * 128 + 128], 128)]
                fmb = [(w_in1[:, hi * 128:hi * 128 + 128], 128),
                       (w_in1[:, 1024 + hi * 128:1024 + hi * 128 + 128], 128),
                       (w_in1[:, 3072 + hi * 128:3072 + hi * 128 + 128], 128)]
                yrow0 = hi * 128
                ng, ngb, ngc = hng, hngb, hi
                for j in range(4):
                    CP("act", c4a[0][:, j * 128:(j + 1) * 128], omlr[:, hi * 128:(hi + 1) * 128], [omlrb], [c4ab[0]])
                    CP("act", c4b[0][:, j * 128:(j + 1) * 128], lbr[:, hi * 128:(hi + 1) * 128], [lbrb], [c4bb[0]])
            else:
                tokb = [(w_in1[:, 4608 + hi * 128:4608 + hi * 128 + 128], 128),
                        (w_in1[:, 5120 + hi * 256:5120 + hi * 256 + 256], 256)]
                fmb = [(w_in1[:, 4096 + hi * 128:4096 + hi * 128 + 128], 128),
                       (w_in1[:, 4608 + hi * 128:4608 + hi * 128 + 128], 128),
                       (w_in1[:, 6144 + hi * 256:6144 + hi * 256 + 256], 256)]
                yrow0 = 1024 + hi * 256
                ng, ngb, ngc = gng, gngb, hi * 2
                for j in range(4):
                    CP("act", c4a[0][:, j * 128:(j + 1) * 128], barow[:, hi * 128:(hi + 1) * 128], [barowb], [c4ab[0]])
            wt, wtb = WP.load_multi(tokb, KC)
            wf, wfb = WP.load_multi(fmb, KC)
            st = {"sp": 0}
            sl0 = scount[0] % 18
            scount[0] += 1
            if half == 0:
                MSET("dve", S32[0][:, 0:dv], 0.0, [S32b[0]])
                MSET("pool", Sring[sl0][:, 0:dv], 0.0, [Sringb[sl0]])
            else:
                S.dma("sp", S32[0][:, 0:dv], stS[:, hidx * 256:hidx * 256 + dv], reads=[stSb[hidx]], writes=[S32b[0]])
                CP("act", Sring[sl0][:, 0:dv], S32[0][:, 0:dv], [S32b[0]], [Sringb[sl0]])
            cur = {"slot": sl0}

            def phaseA1(g):
                r = g % 2
                r3 = g % 3
                gsl = slice(g * 512, (g + 1) * 512)
                for k in range(KC):
                    MM(ps[0][:], wf[:, k, 0:128], u[:, k, gsl], k == 0, k == KC - 1, [wfb[0], ub_[k]], [psb[0]])
                if kind == "h":
                    silu_from_psum(0, qs[r][:], qsb[r])
                else:
                    ACT(qs[r][:], ps[0][:], AF.Copy, [psb[0]], [qsb[r]], scale=qscale)
                for k in range(KC):
                    MM(ps[1][:], wf[:, k, 128:256], u[:, k, gsl], k == 0, k == KC - 1, [wfb[1], ub_[k]], [psb[1]])
                if kind == "h":
                    ACT(tA[0][:], ps[1][:], AF.Sigmoid, [psb[1]], [tAb[0]])
                    TS("dve", kf[r][:], tA[0][:], nomlc[:, hi:hi + 1], omlc[:, hi:hi + 1], ALU.mult, ALU.add,
                       [tAb[0], nomlcb, omlcb], [kfb[r]])
                else:
                    CP("act", kf[r][:], ps[1][:], [psb[1]], [kfb[r]])
                for tl in range(4):
                    tok = (g * 4 + tl) * 128
                    bk = tl % 2
                    for k in range(KC):
                        MM(ps[bk][:, 0:128 + dv], u[:, k, tok:tok + 128], wt[:, k, :], k == 0, k == KC - 1,
                           [ub_[k], wtb[0], wtb[1]], [psb[bk]])
                    CP("act", vt4[r3][:, tl, 0:dv], ps[bk][:, 128:128 + dv], [psb[bk]], [vt4b[r3]])
                    if kind == "h":
                        ACT(sg4[r][:, tl * 128:(tl + 1) * 128], ps[bk][:, 0:128], AF.Sigmoid, [psb[bk]], [sg4b[r]])
                    else:
                        CP("dve", kt4[r][:, tl * 128:(tl + 1) * 128], ps[bk][:, 0:128], [psb[bk]], [kt4b[r]])

            def phaseA2(g):
                r = g % 2
                r3 = g % 3
                if kind == "h":
                    TT("pool", X2[0][:], sg4[r][:], c4a[0][:], ALU.mult, [sg4b[r], c4ab[0]], [X2b[0]])
                    TT("pool", X2[0][:], X2[0][:], c4b[0][:], ALU.add, [X2b[0], c4bb[0]], [X2b[0]])
                    ACT(la4[0][:], X2[0][:], AF.Ln, [X2b[0]], [la4b[0]])
                    TS("pool", kt4[r][:], X2[0][:], -1.0, 1.0, ALU.mult, ALU.add, [X2b[0]], [kt4b[r]])
                else:
                    for tl in range(4):
                        tok = (g * 4 + tl) * 128
                        MM(ps[2][:, tl * 128:(tl + 1) * 128], gaT[0:16, tok:tok + 128], wa2s[0:16, hi * 128:(hi + 1) * 128],
                           True, True, [gaTb, wa2sb], [psb[2]])
                    TT("dve", X1[0][:], ps[2][:], c4a[0][:], ALU.add, [psb[2], c4ab[0]], [X1b[0]])
                    ACT(X1[0][:], X1[0][:], AF.Exp, [X1b[0]], [X1b[0]], scale=-1.0)
                    ACT(X2[0][:], X1[0][:], AF.Ln, [X1b[0]], [X2b[0]], bias=1.0)
                    TS("pool", la4[0][:], X2[0][:], -1.0 / 16.0, None, ALU.mult, None, [X2b[0]], [la4b[0]])
                MM(ps[2][:], Rm, la4[0][:], True, True, [cstb, la4b[0]], [psb[2]])
                ACT(X1[0][:], ps[2][:], AF.Exp, [psb[2]], [X1b[0]])
                TT("pool", kh4[0][:], kt4[r][:], X1[0][:], ALU.mult, [kt4b[r], X1b[0]], [kh4b[0]])
                for tl in range(4):
                    MM(ps[3][:, tl * 128:(tl + 1) * 128], la4[0][:, tl * 128:(tl + 1) * 128], U2, True, True,
                       [la4b[0], cstb], [psb[3]])
                ACT(X2[0][:], ps[3][:], AF.Exp, [psb[3]], [X2b[0]])
                ACT(X1[0][:], ps[3][:], AF.Exp, [psb[3]], [X1b[0]], scale=-1.0)
                TT("dve", qtl4[r][:], qs[r][:], X2[0][:], ALU.mult, [qsb[r], X2b[0]], [qtl4b[r]])
                TT("pool", ktl4[0][:], kf[r][:], X1[0][:], ALU.mult, [kfb[r], X1b[0]], [ktl4b[0]])
                for tl in range(4):
                    MM(ps[4][:, tl * 128:(tl + 1) * 128], ktl4[0][:, tl * 128:(tl + 1) * 128], qtl4[r][:, tl * 128:(tl + 1) * 128],
                       True, True, [ktl4b[0], qtl4b[r]], [psb[4]])
                TT("dve", AT4[r][:], ps[4][:], U4[0][:], ALU.mult, [psb[4], U4b[0]], [AT4b[r]])
                slots = []
                for tl in range(4):
                    pc = (tl % 2) * dv
                    for ch in range(2):
                        MM(ps[6 + ch][:, pc:pc + dv], kh4[0][ch * 64:ch * 64 + 64, tl * 128:(tl + 1) * 128],
                           vt4[r3][ch * 64:ch * 64 + 64, tl, 0:dv], True, True, [kh4b[0], vt4b[r3]], [psb[6 + ch]])
                    for ch in range(2):
                        slots.append(cur["slot"])
                        sp = st["sp"]
                        col = tl * 128 + ch * 64 + 63
                        STT("dve", S32[1 - sp][:, 0:dv], S32[sp][:, 0:dv], X2[0][:, col:col + 1], ps[6 + ch][:, pc:pc + dv],
                            ALU.mult, ALU.add, [S32b[sp], X2b[0], psb[6 + ch]], [S32b[1 - sp]])
                        st["sp"] = 1 - sp
                        ns = scount[0] % 18
                        scount[0] += 1
                        CP("act", Sring[ns][:, 0:dv], S32[1 - sp][:, 0:dv], [S32b[1 - sp]], [Sringb[ns]])
                        cur["slot"] = ns
                return slots

            def phaseB(g, slots):
                r = g % 2
                r3 = g % 3
                gsl = slice(g * 512, (g + 1) * 512)
                gsv = [gs0[0], gs1[0]]
                gsvb = [gs0b[0], gs1b[0]]
                for eh in range(NE):
                    bk = eh % 2
                    for k in range(KC):
                        MM(ps[bk][:], wf[:, k, 256 + eh * 128:256 + (eh + 1) * 128], u[:, k, gsl], k == 0, k == KC - 1,
                           [wfb[2], ub_[k]], [psb[bk]])
                    silu_from_psum(bk, gsv[eh][:], gsvb[eh])
                ohv = [oh0[0], oh1[0]]
                ohvb = [oh0b[0], oh1b[0]]
                obank = [5, 3]
                for eh in range(NE):
                    ob = obank[eh]
                    for tl in range(4):
                        MM(ps[ob][:, tl * 128:(tl + 1) * 128], vt4[r3][:, tl, eh * 128:(eh + 1) * 128], AT4[r][:, tl * 128:(tl + 1) * 128],
                           tl == 0, False, [vt4b[r3], AT4b[r]], [psb[ob]])
                    for tl in range(4):
                        for ch in range(2):
                            sl = slots[tl * 2 + ch]
                            c0 = tl * 128 + ch * 64
                            MM(ps[ob][:, c0:c0 + 64], Sring[sl][:, eh * 128:(eh + 1) * 128], qtl4[r][:, c0:c0 + 64],
                               False, tl == 3 and ch == 1, [Sringb[sl], qtl4b[r]], [psb[ob]])
                    CP("act", ohv[eh][:], ps[ob][:], [psb[ob]], [ohvb[eh]])
                for eh in range(NE):
                    ACT(sqr[eh][:], ohv[eh][:], AF.Square, [ohvb[eh]], [sqrb[eh]])
                    MM(ps[2][:], ones[:], sqr[eh][:], eh == 0, eh == NE - 1, [onesb, sqrb[eh]], [psb[2]])
                ACT(rs[0][:], ps[2][:], AF.Ln, [psb[2]], [rsb[0]], bias=EPS, scale=1.0 / dv)
                ACT(rs[0][:], rs[0][:], AF.Exp, [rsb[0]], [rsb[0]], scale=-0.5)
                yb = [yb0[r], yb1[r]]
                ybb = [yb0b[r], yb1b[r]]
                for eh in range(NE):
                    STT("dve", ohv[eh][:], ohv[eh][:], ng[:, ngc + eh:ngc + eh + 1], rs[0][:], ALU.mult, ALU.mult,
                        [ohvb[eh], ngb, rsb[0]], [ohvb[eh]])
                    TT("pool", yb[eh][:], ohv[eh][:], gsv[eh][:], ALU.mult, [ohvb[eh], gsvb[eh]], [ybb[eh]])
                    r0 = yrow0 + eh * 128
                    S.dma("sp", yT[r0:r0 + 128, t0 + g * 512:t0 + (g + 1) * 512], yb[eh][:], reads=[ybb[eh]],
                          writes=[yTb[(r0 // 128, t0 // 512 + g)]])

            phaseA1(0)
            pend = None
            for g in range(HTT):
                if g + 1 < HTT:
                    phaseA1(g + 1)
                sl = phaseA2(g)
                if pend is not None:
                    phaseB(*pend)
                pend = (g, sl)
            phaseB(*pend)
            if half == 0:
                sp = st["sp"]
                S.dma("sp", stS[:, hidx * 256:hidx * 256 + dv], S32[sp][:, 0:dv], reads=[S32b[sp]], writes=[stSb[hidx]])
        A.release(m)

    def outproj(w_out, gpost, gnext):
        m = A.mark()
        QT = min(1024, TOKC // 2)
        NQ = TOKC // QT
        wsb, wsbb = A.alloc([128, KC, D], BF16, nbuf=8)
        for ip in range(8):
            S.dma("pool", wsb[:, :, ip * 256:(ip + 1) * 256],
                  w_out[:, ip * 256:(ip + 1) * 256].rearrange("(k p) n -> p k n", p=128), writes=[wsbb[ip]])
        y, yb_ = A.alloc([128, KC, QT], BF16, nbuf=KC)
        ots, otbs = [], []
        for st_ in range(2):
            ot, otb = [], []
            for c in range(KC):
                t, b = A.alloc([128, 512], F32)
                ot.append(t[:])
                otb.append(b[0])
            ots.append(ot)
            otbs.append(otb)
        hin, hinb, sq, sqb, ub, ubb, sq2, sq2b = [], [], [], [], [], [], [], []
        for i in range(3):
            t, b = A.alloc([128, 512], F32)
            hin.append(t)
            hinb.append(b[0])
            t, b = A.alloc([128, 512], BF16)
            sq.append(t)
            sqb.append(b[0])
            t, b = A.alloc([128, 512], BF16)
            sq2.append(t)
            sq2b.append(b[0])
            t, b = A.alloc([128, 512], BF16)
            ub.append(t)
            ubb.append(b[0])
        rstd, rstdb = A.alloc([128, 512], F32)
        rstdb = rstdb[0]
        ssbanks = [6, 5]
        pend = None
        idx = 0
        for qi in range(NQ):
            q0 = qi * QT
            for c in range(KC):
                S.dma("sp", y[:, c, :], yT[c * 128:(c + 1) * 128, q0:q0 + QT],
                      reads=[yTb[(c, t)] for t in range(q0 // 512, (q0 + QT) // 512)], writes=[yb_[c]])
            for i in range(QT // 512):
                tt = q0 // 512 + i
                ot, otb, ssb = ots[idx % 2], otbs[idx % 2], ssbanks[idx % 2]
                if pend is not None:
                    ep_head(*pend)
                dfr = None
                for oc in range(KC):
                    bk = oc % 4
                    for k in range(KC):
                        MM(ps[bk][:], wsb[:, k, oc * 128:(oc + 1) * 128], y[:, k, i * 512:(i + 1) * 512], k == 0, k == KC - 1,
                           [wsbb[oc // 2], yb_[k]], [psb[bk]])
                    if dfr is not None:
                        MM(*dfr)
                    CP("act", ot[oc], ps[bk][:], [psb[bk]], [otb[oc]])
                    s = oc % 3
                    ACT(sq2[s][:], ot[oc], AF.Square, [otb[oc]], [sq2b[s]])
                    dfr = (ps[ssb][:], ones[:], sq2[s][:], oc == 0, oc == KC - 1, [onesb, sq2b[s]], [psb[ssb]])
                    if pend is not None:
                        ep_chunk(oc, *pend)
                MM(*dfr)
                if pend is not None:
                    ep_tail(*pend)
                pend = (tt, ot, otb, ssb, gpost, gnext, hin, hinb, sq, sqb, ub, ubb, rstd, rstdb)
                idx += 1
        epilogue(*pend)
        A.release(m)

    def mlp(l, gpost, gnext):
        m = A.mark()
        NT2 = TM // 512
        hid, hidb = A.alloc([128, 64, TM], BF16, nbuf=64)
        WP = WPool(3)
        mu = A.mark()
        for tp in range(TOKC // TM):
            tk0 = tp * TM
            A.release(mu)
            uu, uub = A.alloc([128, KC, TM], BF16, nbuf=KC)
            rl, rlb = [], []
            for i in range(3):
                t, b = A.alloc([128, 512], F32)
                rl.append(t)
                rlb.append(b[0])
            for c in range(KC):
                S.dma("sp", uu[:, c, :], uT[c * 128:(c + 1) * 128, HALO + tk0:HALO + tk0 + TM],
                      reads=[uTb[(c, 1 + tk0 // 512 + t)] for t in range(NT2)], writes=[uub[c]])
            rli = [0]
            for jb in range(32):
                wv, wb = WP.load(w1[l][:, jb * 256:jb * 256 + 256], KC, 256)
                for j2 in range(2):
                    j = jb * 2 + j2
                    for t in range(NT2):
                        bk = (j2 * NT2 + t) % 4
                        for k in range(KC):
                            MM(ps[bk][:], wv[:, k, j2 * 128:(j2 + 1) * 128], uu[:, k, t * 512:(t + 1) * 512], k == 0, k == KC - 1,
                               [wb, uub[k]], [psb[bk]])
                        s = rli[0] % 3
                        rli[0] += 1
                        ACT(rl[s][:], ps[bk][:], AF.Relu, [psb[bk]], [rlb[s]])
                        TT("dve", hid[:, j, t * 512:(t + 1) * 512], rl[s][:], rl[s][:], ALU.mult, [rlb[s]], [hidb[j]])
            A.release(mu)
            zt, ztb, sq, sqb = [], [], [], []
            for i in range(4):
                t, b = A.alloc([128, 512], F32)
                zt.append(t)
                ztb.append(b[0])
                t, b = A.alloc([128, 512], BF16)
                sq.append(t)
                sqb.append(b[0])
            zi = 0
            dfrB = []
            for ip in range(8):
                for jq in range(4):
                    wv, wb = WP.load(w2[l][jq * 2048:(jq + 1) * 2048, ip * 256:ip * 256 + 256], KC, 256)
                    for i2 in range(2):
                        for t in range(NT2):
                            bk = i2 * NT2 + t
                            for jj in range(KC):
                                j = jq * 16 + jj
                                MM(ps[bk][:], wv[:, jj, i2 * 128:(i2 + 1) * 128], hid[:, j, t * 512:(t + 1) * 512],
                                   jq == 0 and jj == 0, jq == 3 and jj == KC - 1, [wb, hidb[j]], [psb[bk]])
                for d_ in dfrB:
                    MM(*d_)
                dfrB = []
                for i2 in range(2):
                    oc = ip * 2 + i2
                    for t in range(NT2):
                        bk = i2 * NT2 + t
                        s = zi % 4
                        zi += 1
                        CP("act", zt[s][:], ps[bk][:], [psb[bk]], [ztb[s]])
                        ACT(sq[s][:], zt[s][:], AF.Square, [ztb[s]], [sqb[s]])
                        dfrB.append((ps[4 + t][:], ones[:], sq[s][:], oc == 0, oc == KC - 1, [onesb, sqb[s]], [psb[4 + t]]))
                        S.dma("sp", zT[oc * 128:(oc + 1) * 128, t * 512:(t + 1) * 512], zt[s][:], reads=[ztb[s]],
                              writes=[zTb[(oc, t)]])
            for d_ in dfrB:
                MM(*d_)
            A.release(mu)
            ot, otb = [], []
            for c in range(KC):
                t, b = A.alloc([128, 512], F32)
                ot.append(t[:])
                otb.append(b[0])
            hin, hinb, sq, sqb, ub, ubb, osb, osbb = [], [], [], [], [], [], [], []
            for i in range(3):
                t, b = A.alloc([128, 512], F32)
                hin.append(t)
                hinb.append(b[0])
            if gnext is not None:
                for i in range(3):
                    t, b = A.alloc([128, 512], BF16)
                    sq.append(t)
                    sqb.append(b[0])
                    t, b = A.alloc([128, 512], BF16)
                    ub.append(t)
                    ubb.append(b[0])
            rstd, rstdb = A.alloc([128, 512], F32)
            rstdb = rstdb[0]
            if gnext is None:
                for i in range(2):
                    t, b = A.alloc([128, 1024], F32)
                    osb.append(t)
                    osbb.append(b[0])
            for t in range(NT2):
                tt = tk0 // 512 + t
                for c in range(KC):
                    S.dma("sp", ot[c], zT[c * 128:(c + 1) * 128, t * 512:(t + 1) * 512], reads=[zTb[(c, t)]], writes=[otb[c]])
                epilogue(tt, ot, otb, 4 + t, gpost, gnext, hin, hinb, sq, sqb, ub, ubb, rstd, rstdb, osb, osbb)
        A.release(m)

    stage0()
    mixer0(0)
    mixer0(1)
    outproj(w_out0, 16, 32)
    mlp(0, 48, 64 if NL > 1 else None)
    if NL > 1:
        if dbg:
            dbu = nc.dram_tensor("dbg_u", [D, NTOT], F32, kind="ExternalOutput").ap()
            for c in range(KC):
                ob_ = Buf()
                outbs.append(ob_)
                S.dma("pool", dbu[c * 128:(c + 1) * 128, :], uT[c * 128:(c + 1) * 128, :],
                      reads=[uTb[(c, t)] for t in range(NTT + 1)], writes=[ob_])
        if dbg:
            dbh = nc.dram_tensor("dbg_h", [D, TOKC], F32, kind="ExternalOutput").ap()
            for c in range(KC):
                ob_ = Buf()
                outbs.append(ob_)
                S.dma("sp", dbh[c * 128:(c + 1) * 128, :], hT[c * 128:(c + 1) * 128, :],
                      reads=[hTb[(c, t)] for t in range(NTT)], writes=[ob_])
        mixer1(0)
        mixer1(1)
        if dbg:
            dby = nc.dram_tensor("dbg_y", [D, TOKC], F32, kind="ExternalOutput").ap()
            for c in range(KC):
                ob_ = Buf()
                outbs.append(ob_)
                S.dma("pool", dby[c * 128:(c + 1) * 128, :], yT[c * 128:(c + 1) * 128, :],
                      reads=[yTb[(c, t)] for t in range(NTT)], writes=[ob_])
        outproj(w_out1, 80, 96)
        mlp(1, 112, None)
    S.op("sp", lambda e: e.nop(), reads=outbs)
    S.emit()
    return nc


def _consts():
    ident = np.eye(128, dtype=np.float32)
    s = np.arange(128)[:, None]
    t = np.arange(128)[None, :]
    same = (s // 64) == (t // 64)
    U2 = (same & (s <= t)).astype(np.float32)
    R = (same & (s > t)).astype(np.float32)
    return np.concatenate([ident, U2, R], axis=1)


def _bias_tables(rel_bias):
    kpos = (np.arange(5)[:, None] * 128 + np.arange(128)[None, :])
    q = np.arange(128)
    rel = (512 + q)[None, None, :] - kpos[:, :, None]
    idx = np.clip(rel, -63, 256) + 63
    qc = (512 + q) // 64
    kc = kpos // 64
    valid = (kc[:, :, None] <= qc[None, None, :]) & (kc[:, :, None] >= qc[None, None, :] - 8)
    out = np.empty((8, 128, 640), np.float32)
    for h in range(8):
        b = rel_bias[h][idx]
        b = np.where(valid, b, np.float32(NEG))
        out[h] = b.transpose(1, 0, 2).reshape(128, 640)
    return out


def _prep_shared(inp, NL):
    f = np.float32
    norm_g = np.asarray(inp["norm_g"], f)
    sh = {}
    sh["w_in0"] = np.ascontiguousarray(np.asarray(inp["even_w_in"], f)[0])
    sh["w_out0"] = np.ascontiguousarray(np.asarray(inp["even_w_out"], f)[0])
    sh["w_in1"] = np.ascontiguousarray(np.asarray(inp["odd_w_in"], f)[0])
    sh["w_out1"] = np.ascontiguousarray(np.asarray(inp["odd_w_out"], f)[0])
    for l in range(2):
        sh["w1_%d" % l] = np.ascontiguousarray(np.asarray(inp["mlp_w1"], f)[l])
        sh["w2_%d" % l] = np.ascontiguousarray(np.asarray(inp["mlp_w2"], f)[l])
    sh["gT"] = np.ascontiguousarray(norm_g.reshape(8, 16, 128).transpose(2, 0, 1).reshape(128, 128))
    sh["convT"] = np.ascontiguousarray(np.asarray(inp["even_conv_w"], f)[0].reshape(3, 8, 128).transpose(2, 0, 1).reshape(128, 24))
    sh["biasT"] = _bias_tables(np.asarray(inp["even_rel_bias"], f)[0])
    sh["cst"] = _consts()
    lb = np.asarray(inp["hgrn_lb"], f)
    sh["lbT"] = np.ascontiguousarray(lb.reshape(2, 8, 128).transpose(2, 0, 1).reshape(128, 16))
    sh["lbrow"] = np.ascontiguousarray(np.broadcast_to(lb.reshape(1, 2048), (128, 2048)))
    sh["hngT"] = np.ascontiguousarray(np.asarray(inp["hgrn_norm_g"], f)[0].reshape(8, 128).T)
    sh["gngT"] = np.ascontiguousarray(np.asarray(inp["gla_norm_g"], f)[0].reshape(8, 128).T)
    sh["wa2"] = np.ascontiguousarray(np.asarray(inp["gla_wa2"], f)[0])
    sh["barow"] = np.ascontiguousarray(np.broadcast_to(np.asarray(inp["gla_ba"], f)[0].reshape(1, 512), (128, 512)))
    return sh


def run(inp, NL=2, tokc=None, trace=False, dbg=False):
    x = np.asarray(inp["x"], np.float32)
    B, SQ, _ = x.shape
    tokc = tokc or SQ
    nc = build(tokc, NL, dbg)
    sh = _prep_shared(inp, NL)
    in_maps = []
    for b in range(B):
        m = dict(sh)
        m["x"] = np.ascontiguousarray(x[b, :tokc])
        in_maps.append(m)
    res = run_bass_kernel_spmd(nc, in_maps, core_ids=list(range(B)), trace=trace)
    outs = np.stack([np.asarray(r["out"]) for r in res.results], axis=0)
    if dbg:
        return outs.astype(np.float32), res, [(np.asarray(r["dbg_y"]), np.asarray(r["dbg_u"]), np.asarray(r["dbg_h"])) for r in res.results]
    return outs.astype(np.float32), res


def kernel(**inputs):
    o, _ = run(inputs)
    return o
```
